# Optimizing a Trainium2 kernel written in Bass

```python
import jax, jax.numpy as jnp
from jax import lax
import numpy as np

D_MODEL = 2048
BATCH = 2
SEQ = 16384
DEPTH = 1

GLA_HEADS = 4
GLA_DK = D_MODEL // 2 // GLA_HEADS
GLA_DV = D_MODEL // GLA_HEADS
GLA_GATE_RANK = 16
GLA_TAU = 16.0
GLA_CHUNK = 64

SWA_HEADS = 32
SWA_KV_HEADS = 4
SWA_HEAD_DIM = 64
SWA_GROUP = SWA_HEADS // SWA_KV_HEADS
WINDOW = 128

D_FF = 4 * D_MODEL

DEEPNORM_ALPHA = (2.0 * DEPTH) ** 0.25
DEEPNORM_BETA = (8.0 * DEPTH) ** -0.25
LN_EPS = 1e-5
RMS_EPS = 1e-6

SPLITS = (
    GLA_HEADS * GLA_DK,
    GLA_HEADS * GLA_DK,
    GLA_HEADS * GLA_DV,
    GLA_HEADS * GLA_DV,
    GLA_GATE_RANK,
    SWA_HEADS * SWA_HEAD_DIM,
    SWA_KV_HEADS * SWA_HEAD_DIM,
    SWA_KV_HEADS * SWA_HEAD_DIM,
    D_MODEL,
    D_MODEL,
)
V_SEGMENTS = (2, 7)

kernel_name = "hybrid_gla_swa_deepnorm_layer"


def layer_norm(x, g, b):
    xf = x.astype(jnp.float32)
    mu = jnp.mean(xf, axis=-1, keepdims=True)
    var = jnp.mean(jnp.square(xf - mu), axis=-1, keepdims=True)
    y = (xf - mu) * lax.rsqrt(var + LN_EPS) * g.astype(jnp.float32) + b.astype(jnp.float32)
    return y.astype(x.dtype)


def alibi_slopes(n_heads):
    h = np.arange(1, n_heads + 1, dtype=np.float32)
    return jnp.asarray(2.0 ** (-8.0 * h / n_heads), dtype=jnp.float32)


def gla_chunked(q, k, v, log_a):
    B, S, H, DK = q.shape
    DV = v.shape[-1]
    C = GLA_CHUNK
    nc = S // C
    q = q.reshape(B, nc, C, H, DK)
    k = k.reshape(B, nc, C, H, DK)
    v = v.reshape(B, nc, C, H, DV)
    b = jnp.cumsum(log_a.reshape(B, nc, C, H, DK), axis=2)
    b_last = b[:, :, -1]
    q_in = q * jnp.exp(b)
    k_in = k * jnp.exp(-b)
    k_dec = k * jnp.exp(b_last[:, :, None] - b)
    decay = jnp.exp(b_last)
    causal = jnp.tril(jnp.ones((C, C), dtype=bool))
    scores = jnp.einsum('bnqhd,bnshd->bnhqs', q_in, k_in)
    scores = jnp.where(causal, scores, 0.0)
    o_intra = jnp.einsum('bnhqs,bnshv->bnqhv', scores, v)

    def step(state, xs):
        qc, kc, vc, dc = xs
        o = jnp.einsum('bqhd,bhdv->bqhv', qc, state)
        state = dc[..., None] * state + jnp.einsum('bshd,bshv->bhdv', kc, vc)
        return state, o

    xs = (jnp.moveaxis(q_in, 1, 0), jnp.moveaxis(k_dec, 1, 0),
          jnp.moveaxis(v, 1, 0), jnp.moveaxis(decay, 1, 0))
    state0 = jnp.zeros((B, H, DK, DV), dtype=q.dtype)
    _, o_inter = lax.scan(step, state0, xs)
    o = o_intra + jnp.moveaxis(o_inter, 0, 1)
    return o.reshape(B, S, H, DV)


def swa_banded(q, k, v, sinks):
    B, S, H, hd = q.shape
    W = WINDOW
    nb = S // W
    qb = q.reshape(B, nb, W, SWA_KV_HEADS, SWA_GROUP, hd)
    pad = ((0, 0), (W, 0), (0, 0), (0, 0))
    kp = jnp.pad(k, pad).reshape(B, nb + 1, W, SWA_KV_HEADS, hd)
    vp = jnp.pad(v, pad).reshape(B, nb + 1, W, SWA_KV_HEADS, hd)
    kb = jnp.concatenate([kp[:, :-1], kp[:, 1:]], axis=2)
    vb = jnp.concatenate([vp[:, :-1], vp[:, 1:]], axis=2)
    logits = jnp.einsum('bnqkgd,bnskd->bnkgqs', qb, kb).astype(jnp.float32) * (hd ** -0.5)
    qi = jnp.arange(W)[:, None]
    sj = jnp.arange(2 * W)[None, :]
    dist = qi - sj + W
    blk = jnp.arange(nb)[:, None, None]
    valid = (dist >= 0) & (dist < W) & (blk * W + sj - W >= 0)
    slopes = alibi_slopes(SWA_HEADS).reshape(SWA_KV_HEADS, SWA_GROUP)
    logits = logits - slopes[:, :, None, None] * dist.astype(jnp.float32)
    logits = jnp.where(valid[None, :, None, None], logits, jnp.finfo(jnp.float32).min)
    sink = sinks.astype(jnp.float32).reshape(SWA_KV_HEADS, SWA_GROUP)[None, None, :, :, None, None]
    m = jnp.maximum(jnp.max(logits, axis=-1, keepdims=True), sink)
    p = jnp.exp(logits - m)
    probs = p / (jnp.sum(p, axis=-1, keepdims=True) + jnp.exp(sink - m))
    o = jnp.einsum('bnkgqs,bnskd->bnqkgd', probs.astype(v.dtype), vb)
    return o.reshape(B, S, H * hd)


def setup_inputs(seed: int = 0) -> dict:
    key = jax.random.key(seed)
    ks = jax.random.split(key, 16)
    f32 = jnp.float32
    n_in = int(sum(SPLITS))
    col_scale = np.concatenate([
        np.full((s,), DEEPNORM_BETA if i in V_SEGMENTS else 1.0, dtype=np.float32)
        for i, s in enumerate(SPLITS)])
    x = jax.random.normal(ks[0], (BATCH, SEQ, D_MODEL), f32)
    w_in = jax.random.normal(ks[1], (D_MODEL, n_in), f32) * (D_MODEL ** -0.5) * jnp.asarray(col_scale)
    w_alpha_up = jax.random.normal(ks[2], (GLA_GATE_RANK, GLA_HEADS * GLA_DK), f32) * (GLA_GATE_RANK ** -0.5)
    b_alpha = 0.1 * jax.random.normal(ks[3], (GLA_HEADS * GLA_DK,), f32)
    gla_norm_w = 1.0 + 0.02 * jax.random.normal(ks[4], (GLA_DV,), f32)
    attn_sinks = 0.5 * jax.random.normal(ks[5], (SWA_HEADS,), f32)
    w_branch_gla = jax.random.normal(ks[6], (GLA_HEADS * GLA_DV, D_MODEL), f32) * ((GLA_HEADS * GLA_DV) ** -0.5) * DEEPNORM_BETA
    w_branch_swa = jax.random.normal(ks[7], (SWA_HEADS * SWA_HEAD_DIM, D_MODEL), f32) * ((SWA_HEADS * SWA_HEAD_DIM) ** -0.5) * DEEPNORM_BETA
    w_out = jax.random.normal(ks[8], (D_MODEL, D_MODEL), f32) * (D_MODEL ** -0.5) * DEEPNORM_BETA
    ln1_g = 1.0 + 0.02 * jax.random.normal(ks[9], (D_MODEL,), f32)
    ln1_b = 0.02 * jax.random.normal(ks[10], (D_MODEL,), f32)
    w_ff_up = jax.random.normal(ks[11], (D_MODEL, D_FF), f32) * (D_MODEL ** -0.5)
    w_ff_down = jax.random.normal(ks[12], (D_FF, D_MODEL), f32) * (D_FF ** -0.5) * DEEPNORM_BETA
    ln2_g = 1.0 + 0.02 * jax.random.normal(ks[13], (D_MODEL,), f32)
    ln2_b = 0.02 * jax.random.normal(ks[14], (D_MODEL,), f32)
    return {"x": x, "w_in": w_in, "w_alpha_up": w_alpha_up, "b_alpha": b_alpha,
            "gla_norm_w": gla_norm_w, "attn_sinks": attn_sinks,
            "w_branch_gla": w_branch_gla, "w_branch_swa": w_branch_swa, "w_out": w_out,
            "ln1_g": ln1_g, "ln1_b": ln1_b, "w_ff_up": w_ff_up, "w_ff_down": w_ff_down,
            "ln2_g": ln2_g, "ln2_b": ln2_b}


def reference(x, w_in, w_alpha_up, b_alpha, gla_norm_w, attn_sinks, w_branch_gla,
              w_branch_swa, w_out, ln1_g, ln1_b, w_ff_up, w_ff_down, ln2_g, ln2_b):
    B, S, _ = x.shape
    f32 = jnp.float32
    offsets = [int(o) for o in np.cumsum(SPLITS)[:-1]]
    for _layer in range(DEPTH):
        proj = x @ w_in
        (g_q, g_k, g_v, g_out, g_lr, s_q, s_k, s_v,
         gate_gla, gate_swa) = jnp.split(proj, offsets, axis=-1)

        log_a = jax.nn.log_sigmoid((g_lr @ w_alpha_up + b_alpha).astype(f32)) / GLA_TAU
        q_a = g_q.astype(f32).reshape(B, S, GLA_HEADS, GLA_DK) * (GLA_DK ** -0.5)
        k_a = g_k.astype(f32).reshape(B, S, GLA_HEADS, GLA_DK)
        v_a = g_v.astype(f32).reshape(B, S, GLA_HEADS, GLA_DV)
        o_a = gla_chunked(q_a, k_a, v_a, log_a.reshape(B, S, GLA_HEADS, GLA_DK))
        o_a = o_a * lax.rsqrt(jnp.mean(jnp.square(o_a), axis=-1, keepdims=True) + RMS_EPS) * gla_norm_w.astype(f32)
        o_a = (o_a.reshape(B, S, GLA_HEADS * GLA_DV) * jax.nn.silu(g_out.astype(f32))).astype(x.dtype)
        y_a = o_a @ w_branch_gla

        q_b = s_q.reshape(B, S, SWA_HEADS, SWA_HEAD_DIM)
        k_b = s_k.reshape(B, S, SWA_KV_HEADS, SWA_HEAD_DIM)
        v_b = s_v.reshape(B, S, SWA_KV_HEADS, SWA_HEAD_DIM)
        o_b = swa_banded(q_b, k_b, v_b, attn_sinks)
        y_b = o_b @ w_branch_swa

        merged = jax.nn.sigmoid(gate_gla) * y_a + jax.nn.sigmoid(gate_swa) * y_b
        mix = merged @ w_out
        x = layer_norm(DEEPNORM_ALPHA * x + mix, ln1_g, ln1_b)

        h = jnp.square(jax.nn.relu(x @ w_ff_up))
        x = layer_norm(DEEPNORM_ALPHA * x + h @ w_ff_down, ln2_g, ln2_b)
    return x
```

```python
import os
import numpy as np
import ml_dtypes
from contextlib import ExitStack
import concourse.bass as bass
import concourse.mybir as mybir
from concourse.alu_op_type import AluOpType as ALU
from concourse.bass_utils import run_bass_kernel_spmd

F32 = mybir.dt.float32
BF16 = mybir.dt.bfloat16
AF = mybir.ActivationFunctionType

D = 2048
T = 512
NIN = 12816
OFF_GQ, OFF_GK, OFF_GV, OFF_GO, OFF_LR = 0, 1024, 2048, 4096, 6144
OFF_SQ, OFF_SK, OFF_SV, OFF_GA, OFF_GB = 6160, 8208, 8464, 8720, 10768
ALPHA = 2.0 ** 0.25
LN_EPS = 1e-5
RMS_EPS = 1e-6
ENGS = ("pe", "act", "dve", "pool", "sp")


class Buf:
    __slots__ = ("name", "w", "r", "dsem")

    def __init__(self, name):
        self.name = name
        self.w = None
        self.r = {}
        self.dsem = None


class Prog:
    def __init__(self, nc, stack):
        self.nc = nc
        self.stack = stack
        self.ops = {e: [] for e in ENGS}
        self.count = {}
        self.seen = {e: {} for e in ENGS}
        self.esem = {}
        self.nops = 0
        self.limit = None
        for e in ENGS:
            s = stack.enter_context(nc.semaphore("sem_" + e))
            self.esem[e] = s
            self.count[s] = 0

    def new_sem(self, name):
        s = self.stack.enter_context(self.nc.semaphore(name))
        self.count[s] = 0
        return s

    def _deps(self, eng, reads, writes):
        need = {}
        pe = eng == "pe"
        for b in reads:
            if b.w is not None:
                s, v, e = b.w
                if not (pe and e == "pe"):
                    if need.get(s, 0) < v:
                        need[s] = v
        for b in writes:
            if b.w is not None:
                s, v, e = b.w
                if not (pe and e == "pe"):
                    if need.get(s, 0) < v:
                        need[s] = v
            for s, (v, e) in b.r.items():
                if not (pe and e == "pe"):
                    if need.get(s, 0) < v:
                        need[s] = v
        waits = []
        seen = self.seen[eng]
        for s, v in need.items():
            if seen.get(s, 0) < v:
                waits.append((s, v))
                seen[s] = v
        return waits

    def _mark(self, tok, reads, writes):
        s, v, e = tok
        for b in reads:
            old = b.r.get(s)
            if old is None or old[0] < v:
                b.r[s] = (v, e)
        for b in writes:
            b.w = tok
            b.r = {}

    def op(self, eng, fn, reads=(), writes=(), sig=True):
        waits = self._deps(eng, reads, writes)
        s = self.esem[eng]
        if sig:
            self.count[s] += 1
            inc = (s, 1)
            tok = (s, self.count[s], eng)
        else:
            inc = None
            tok = (s, self.count[s] + 1, eng)
        self._mark(tok, reads, writes)
        self.nops += 1
        if self.limit is None or self.nops <= self.limit:
            self.ops[eng].append((fn, waits, inc))

    def dma(self, eng, fns, buf, reads=(), writes=()):
        if buf.dsem is None:
            buf.dsem = self.new_sem("d_" + buf.name)
        waits = self._deps(eng, reads, writes)
        s = buf.dsem
        self.nops += 1
        for i, fn in enumerate(fns):
            self.count[s] += 16
            if self.limit is None or self.nops <= self.limit:
                self.ops[eng].append((fn, waits if i == 0 else [], (s, 16)))
        self._mark((s, self.count[s], "dma"), reads, writes)

    def final_wait(self, eng, bufs):
        waits = self._deps(eng, bufs, bufs)
        if self.limit is None:
            self.ops[eng].append((None, waits, None))
        print("PROG nops", self.nops, {e: len(v) for e, v in self.ops.items()})

    def emit(self, block):
        handles = {"pe": "tensor", "act": "scalar", "dve": "vector", "pool": "gpsimd", "sp": "sync"}

        def make(e):
            def body(h):
                for fn, waits, inc in self.ops[e]:
                    for s, v in waits:
                        h.wait_ge(s, v)
                    if fn is None:
                        continue
                    ins = fn(h)
                    if inc is not None:
                        ins.then_inc(inc[0], inc[1])
            return body

        for e in ENGS:
            getattr(block, handles[e])(make(e))


class Region:
    def __init__(self, nc, stack, name, nbytes, gran):
        self.t = stack.enter_context(nc.sbuf_tensor(name, [128, nbytes // 2], BF16))
        self.gran = gran
        self.bufs = [Buf(f"{name}{i}") for i in range(nbytes // gran)]
        self.nbytes = nbytes

    def view(self, off, shape, dtype):
        esz = 4 if dtype == F32 else 2
        n = 1
        for s in shape:
            n *= s
        nb = n * esz
        assert off % 4 == 0 and off + nb <= self.nbytes, (off, nb, self.nbytes)
        ap = self.t[:, off // 2:(off + nb) // 2]
        if dtype == F32:
            ap = ap.bitcast(F32)
        if len(shape) == 2:
            ap = ap.rearrange("p (a b) -> p a b", a=shape[0])
        elif len(shape) == 3:
            ap = ap.rearrange("p (a b c) -> p a b c", a=shape[0], b=shape[1])
        bufs = self.bufs[off // self.gran:(off + nb - 1) // self.gran + 1]
        return ap, bufs


def build(NT_MAIN, NT_PRE):
    nc = bass.Bass("TRN2", target_bir_lowering=False)
    TOK = NT_MAIN * T
    PTOK = max(NT_PRE, 1) * T

    def din(name, shape, dt=F32):
        return nc.dram_tensor(name, shape, dt, kind="ExternalInput").ap()

    x = din("x", [TOK, D])
    xprev = din("xprev", [PTOK, D])
    xhalo = din("xhalo", [128, D])
    hvalid = din("hvalid", [128, 1])
    w_in = din("w_in", [D, NIN])
    w_alpha_up = din("w_alpha_up", [16, 1024])
    b_alpha = din("b_alpha", [1024])
    gla_norm_w = din("gla_norm_w", [512])
    attn_sinks = din("attn_sinks", [32])
    w_bg = din("w_branch_gla", [D, D])
    w_bs = din("w_branch_swa", [D, D])
    w_out = din("w_out", [D, D])
    ln1_g = din("ln1_g", [D])
    ln1_b = din("ln1_b", [D])
    w_up = din("w_ff_up", [D, 4 * D])
    w_dn = din("w_ff_down", [4 * D, D])
    ln2_g = din("ln2_g", [D])
    ln2_b = din("ln2_b", [D])
    swa_mask = din("swa_mask", [128, 4, 2048], BF16)
    out = nc.dram_tensor("out", [TOK, D], F32, kind="ExternalOutput").ap()

    with ExitStack() as st:
        P = Prog(nc, st)

        def sb(name, shape, dt):
            return st.enter_context(nc.sbuf_tensor(name, shape, dt))

        ident_f = sb("ident_f", [128, 128], F32)
        ident_b = sb("ident_b", [128, 128], BF16)
        cmask = sb("cmask", [128, 128], F32)
        ones_f = sb("ones_f", [128, 128], F32)
        S_f = sb("S_f", [128, 8, 512], F32)
        S_b = sb("S_b", [128, 8, 512], BF16)
        w_lr = sb("w_lr", [128, 16, 16], BF16)
        w_au = sb("w_au", [16, 1024], BF16)
        nba = sb("nba", [128, 8], F32)
        gw_bc = sb("gw_bc", [128, 512], F32)
        sinkexp = sb("sinkexp", [128, 32], F32)
        hval = sb("hval", [128, 1], F32)
        gb = sb("gb", [128, 2, D], F32)
        kT_carry = sb("kT_carry", [128, 4, 128], BF16)
        v_carry = sb("v_carry", [128, 4, 65], BF16)
        lrT = sb("lrT", [16, T], BF16)
        rtmp = sb("rtmp", [128, 2, 512], F32)
        small = sb("small", [128, 64], F32)
        stats = sb("stats", [128, 4, 6], F32)
        xT = sb("xT", [128, 16, T], BF16)
        RX = Region(nc, st, "RX", 32 * 1024, 1024)
        RB = Region(nc, st, "RB", 96 * 1024, 1024)

        B_ident_f, B_ident_b, B_cmask, B_ones = Buf("ident_f"), Buf("ident_b"), Buf("cmask"), Buf("ones")
        B_S_f = [Buf(f"S_f{i}") for i in range(8)]
        B_S_b = [Buf(f"S_b{i}") for i in range(8)]
        B_wlr, B_wau, B_nba, B_gw, B_sink, B_hval = (Buf("wlr"), Buf("wau"), Buf("nba"), Buf("gw"),
                                                      Buf("sink"), Buf("hval"))
        B_gb, B_kc, B_vc, B_lrT = Buf("gb"), Buf("kc"), Buf("vc"), Buf("lrT")
        B_rtmp = [Buf("rtmp0"), Buf("rtmp1")]
        B_small = {}
        B_stats = Buf("stats")
        B_xT = [Buf(f"xT{i}") for i in range(16)]
        B_xload, B_xstore, B_gbload, B_mskload = Buf("xload"), Buf("xstore"), Buf("gbload"), Buf("mskload")
        B_xstore4 = [Buf(f"xstore{i}") for i in range(4)]

        def smallv(name, idx, n=1):
            if name not in B_small:
                B_small[name] = Buf("sm_" + name)
            return small[:, idx:idx + n], B_small[name]

        psM = [st.enter_context(nc.psum_tensor(f"psM{i}", [128, 512], F32)) for i in range(4)]
        psA = [st.enter_context(nc.psum_tensor(f"psA{i}", [128, 512], F32)) for i in range(3)]
        psT = st.enter_context(nc.psum_tensor("psT", [128, 1024], BF16))
        B_psM = [Buf(f"psM{i}") for i in range(4)]
        B_psA = [Buf(f"psA{i}") for i in range(3)]
        B_psT = Buf("psT")
        psT_f = psT[:, :].bitcast(F32)
        psA_bf = [psA[i][:, :].bitcast(BF16) for i in range(3)]

        def mm(o, lhsT, rhs, start, stop, reads, writes, sig):
            P.op("pe", lambda h: h.matmul(o, lhsT=lhsT, rhs=rhs, start=start, stop=stop),
                 reads=reads, writes=writes, sig=sig)

        def tr(o, in_, idt, reads, writes, sig):
            P.op("pe", lambda h: h.transpose(out=o, in_=in_, identity=idt), reads=reads, writes=writes, sig=sig)

        def act(o, in_, func, reads, writes, scale=None, bias=None, accum=None):
            kw = {}
            if scale is not None:
                kw["scale"] = scale
            if bias is not None:
                kw["bias"] = bias
            if accum is not None:
                kw["accum_out"] = accum
            P.op("act", lambda h: h.activation(out=o, in_=in_, func=func, **kw), reads=reads, writes=writes)

        def tt(o, a, b, op, reads, writes, eng="dve"):
            P.op(eng, lambda h: h.tensor_tensor(out=o, in0=a, in1=b, op=op), reads=reads, writes=writes)

        def stt(o, a, scalar, b, op0, op1, reads, writes):
            P.op("dve", lambda h: h.scalar_tensor_tensor(out=o, in0=a, scalar=scalar, in1=b, op0=op0, op1=op1),
                 reads=reads, writes=writes)

        def ts(o, a, s1, s2, op0, op1, reads, writes):
            if op1 is None:
                P.op("dve", lambda h: h.tensor_scalar(out=o, in0=a, scalar1=s1, scalar2=None, op0=op0),
                     reads=reads, writes=writes)
            else:
                P.op("dve", lambda h: h.tensor_scalar(out=o, in0=a, scalar1=s1, scalar2=s2, op0=op0, op1=op1),
                     reads=reads, writes=writes)

        def cp(eng, o, a, reads, writes):
            if eng == "act":
                P.op("act", lambda h: h.activation(out=o, in_=a, func=AF.Copy), reads=reads, writes=writes)
            else:
                P.op("dve", lambda h: h.tensor_copy(out=o, in_=a), reads=reads, writes=writes)

        evac_rr = [0]

        def cp_rr(o, a, reads, writes):
            evac_rr[0] ^= 1
            cp("act" if evac_rr[0] else "dve", o, a, reads, writes)

        NSLOT = 4
        slot_ap = []
        slot_buf = []
        for i in range(NSLOT):
            ap_, bufs_ = RB.view((8 + i) * 8192, [8, 512], BF16)
            slot_ap.append(ap_)
            slot_buf.append(bufs_)
        wstate = [0]

        def wget(w, k0, pieces, nk=8):
            i = wstate[0] % NSLOT
            wstate[0] += 1
            sl, bf = slot_ap[i], slot_buf[i]
            fns = []
            co = 0
            for (c0, n) in pieces:
                for kh in range(0, nk, 4):
                    kn = min(4, nk - kh)
                    src = w[(k0 + kh) * 128:(k0 + kh + kn) * 128, c0:c0 + n].rearrange("(k p) c -> p k c", p=128)
                    dst = sl[:, kh:kh + kn, co:co + n]
                    fns.append(lambda h, dst=dst, src=src: h.dma_start(out=dst, in_=src))
                co += n
            P.dma("pool", fns, bf[0], writes=bf)
            return sl, bf

        def lin_fm(w, pieces, nchunk, rhs_fn, evac, nkb=2, resident=None):
            for kb in range(nkb):
                if resident is None:
                    sl, bf = wget(w, kb * 8, pieces)
                for oc in range(nchunk):
                    for k in range(8):
                        kk = kb * 8 + k
                        r_ap, r_bufs = rhs_fn(kk)
                        if resident is None:
                            l_ap, l_bufs = sl[:, k, oc * 128:(oc + 1) * 128], bf[k:k + 1]
                        else:
                            l_ap, l_bufs = resident(kk, oc)
                        n = r_ap.shape[-1]
                        mm(psM[oc][:, 0:n], l_ap, r_ap, kk == 0, kk == nkb * 8 - 1,
                           reads=l_bufs + r_bufs, writes=[B_psM[oc]],
                           sig=(kk == nkb * 8 - 1) or (k == 7 and oc == nchunk - 1))
            for oc in range(nchunk):
                evac(oc, psM[oc], B_psM[oc])

        def lin_tm(w, c0, ncols, lhs_fn, evac, nkb=2, nsub=4, resident=None):
            for kb in range(nkb):
                if resident is None:
                    sl, bf = wget(w, kb * 8, [(c0, ncols)])
                for sub in range(nsub):
                    for k in range(8):
                        kk = kb * 8 + k
                        l_ap, l_bufs = lhs_fn(kk, sub)
                        if resident is None:
                            r_ap, r_bufs = sl[:, k, 0:ncols], bf[k:k + 1]
                        else:
                            r_ap, r_bufs = resident(kk)
                        mm(psM[sub][:, 0:ncols], l_ap, r_ap, kk == 0, kk == nkb * 8 - 1,
                           reads=l_bufs + r_bufs, writes=[B_psM[sub]],
                           sig=(kk == nkb * 8 - 1) or (k == 7 and sub == nsub - 1))
            for sub in range(nsub):
                evac(sub, psM[sub], B_psM[sub])

        P.op("pool", lambda h: h.memset(ident_f[:], 1.0), writes=[B_ident_f])
        P.op("pool", lambda h: h.affine_select(out=ident_f[:], in_=ident_f[:], pattern=[[-1, 128]],
                                               compare_op=ALU.is_equal, fill=0.0, base=0, channel_multiplier=1),
             reads=[B_ident_f], writes=[B_ident_f])
        P.op("pool", lambda h: h.memset(cmask[:], 1.0), writes=[B_cmask])
        P.op("pool", lambda h: h.affine_select(out=cmask[:], in_=cmask[:], pattern=[[1, 128]],
                                               compare_op=ALU.is_ge, fill=0.0, base=0, channel_multiplier=-1),
             reads=[B_cmask], writes=[B_cmask])
        P.op("pool", lambda h: h.memset(ones_f[:], 1.0), writes=[B_ones])
        cp("dve", ident_b[:], ident_f[:], [B_ident_f], [B_ident_b])
        for i in range(8):
            P.op("dve", lambda h, i=i: h.memset(S_f[:, i, :], 0.0), writes=[B_S_f[i]])
            P.op("dve", lambda h, i=i: h.memset(S_b[:, i, :], 0.0), writes=[B_S_b[i]])
        P.dma("pool", [lambda h: h.dma_start(out=w_lr[:], in_=w_in[:, OFF_LR:OFF_LR + 16].rearrange("(k p) c -> p k c", p=128))],
              B_wlr, writes=[B_wlr])
        P.dma("pool", [lambda h: h.dma_start(out=w_au[:], in_=w_alpha_up[:, :])], B_wau, writes=[B_wau])
        P.dma("sp", [lambda h: h.dma_start(out=nba[:], in_=b_alpha.rearrange("(c p) -> p c", p=128),
                                            allow_slow_non_contiguous=True)], B_nba, writes=[B_nba])
        ts(nba[:], nba[:], -1.0, None, ALU.mult, None, [B_nba], [B_nba])
        P.dma("sp", [lambda h: h.dma_start(out=gw_bc[:], in_=gla_norm_w.partition_broadcast(128))], B_gw, writes=[B_gw])
        P.dma("sp", [lambda h: h.dma_start(out=sinkexp[:], in_=attn_sinks.partition_broadcast(128))], B_sink, writes=[B_sink])
        act(sinkexp[:], sinkexp[:], AF.Exp, [B_sink], [B_sink])
        P.dma("sp", [lambda h: h.dma_start(out=hval[:], in_=hvalid[:, :])], B_hval, writes=[B_hval])
        g1T = sb("g1T", [128, 16], F32)
        b1T = sb("b1T", [128, 16], F32)
        B_g1T = Buf("g1T")
        P.dma("sp", [lambda h: h.dma_start(out=g1T[:], in_=ln1_g.rearrange("(c p) -> p c", p=128), allow_slow_non_contiguous=True),
                     lambda h: h.dma_start(out=b1T[:], in_=ln1_b.rearrange("(c p) -> p c", p=128), allow_slow_non_contiguous=True)],
              B_g1T, writes=[B_g1T])

        xs_ap, xs_bufs = RX.view(0, [4, D], F32)

        def load_x_tile(src, ntok=T):
            ns = ntok // 128
            P.dma("sp", [lambda h, s=s: h.dma_start(out=xs_ap[:, s, :], in_=src[s * 128:(s + 1) * 128, :]) for s in range(ns)],
                  B_xload, writes=xs_bufs[:8 * ns])

        def transpose_to_xT(ns=4, src=None, src_bufs=None, affine=False):
            if src is None:
                src, src_bufs = xs_ap, xs_bufs
            for c in range(16):
                pa = psA[c % 2]
                bpa = B_psA[c % 2]
                for s in range(ns):
                    tr(pa[:, s * 128:(s + 1) * 128], src[:, s, c * 128:(c + 1) * 128], ident_f[:],
                       reads=src_bufs[s * 8 + c // 2: s * 8 + c // 2 + 1] + [B_ident_f], writes=[bpa], sig=(s == ns - 1))
                if not affine:
                    cp_rr(xT[:, c, 0:ns * 128], pa[:, 0:ns * 128], [bpa], [B_xT[c]])
                elif c % 2 == 0:
                    act(xT[:, c, :], pa[:, :], AF.Identity, [bpa, B_g1T], [B_xT[c]], scale=g1T[:, c:c + 1], bias=b1T[:, c:c + 1])
                else:
                    ts(xT[:, c, :], pa[:, :], g1T[:, c:c + 1], b1T[:, c:c + 1], ALU.mult, ALU.add, [bpa, B_g1T], [B_xT[c]])

        def xT_rhs(k):
            return xT[:, k, :], [B_xT[k]]

        def xT_lhs(k, sub):
            return xT[:, k, sub * 128:(sub + 1) * 128], [B_xT[k]]

        class GV:
            pass

        def make_gla_views(reg, base):
            g = GV()
            g.E1, g.bE1 = reg.view(base + 0, [2, 512], F32)
            g.E2, g.bE2 = reg.view(base + 4096, [2, 512], F32)
            g.la, g.bla = reg.view(base + 8192, [512], F32)
            g.bp, g.bbp = reg.view(base + 10240, [512], F32)
            g.dec, g.bdec = reg.view(base + 12288, [2, 4], F32)
            g.qinT, g.bqin = reg.view(base + 14336, [2, 512], BF16)
            g.kinT, g.bkin = reg.view(base + 16384, [2, 512], BF16)
            g.kdecT, g.bkdecT = reg.view(base + 18432, [2, 512], BF16)
            g.kdec_tm, g.bkdtm = reg.view(base + 20480, [2, 4, 128], BF16)
            g.v_tm, g.bvtm = reg.view(base + 22528, [4, 512], BF16)
            g.gsw, g.bgsw = reg.view(base + 26624, [4, 512], BF16)
            g.o_g, g.bog = reg.view(base + 30720, [512], BF16)
            g.sT, g.bsT = reg.view(base + 31744, [128], BF16)
            return g

        GSETS = [make_gla_views(RX, 0), make_gla_views(RB, 16384)]
        ssq, b_ssq = smallv("ssq", 0)
        rst, b_rst = smallv("rst", 1)

        def gla_lr():
            for k in range(16):
                mm(psA[2][0:16, :], w_lr[:, k, :], xT[:, k, :], k == 0, k == 15,
                   reads=[B_wlr, B_xT[k]], writes=[B_psA[2]], sig=(k == 15))
            cp("act", lrT[:], psA[2][0:16, :], [B_psA[2]], [B_lrT])

        def gla_decay(g, h_, c2):
            gc = h_ * 2 + c2
            zb, bzb = (psA[2][:, :], B_psA[2]) if c2 == 0 else (psT_f, B_psT)
            mm(zb, w_au[0:16, gc * 128:(gc + 1) * 128], lrT[0:16, :], True, True,
               reads=[B_wau, B_lrT], writes=[bzb], sig=True)
            act(g.la, zb, AF.Exp, [bzb, B_nba], g.bla, scale=-1.0, bias=nba[:, gc:gc + 1])
            act(g.la, g.la, AF.Ln, g.bla, g.bla, bias=1.0)
            for ch in range(4):
                P.op("dve", lambda h, ch=ch: h.tensor_tensor_scan(out=g.bp[:, ch * 128:(ch + 1) * 128], data0=ones_f[:, :],
                                                                  data1=g.la[:, ch * 128:(ch + 1) * 128], initial=0.0,
                                                                  op0=ALU.mult, op1=ALU.add),
                     reads=g.bla + [B_ones], writes=g.bbp)
            act(g.E1[:, c2, :], g.bp, AF.Exp, g.bbp, g.bE1[c2 * 2:c2 * 2 + 2], scale=-1.0 / 16)
            act(g.E2[:, c2, :], g.bp, AF.Exp, g.bbp, g.bE2[c2 * 2:c2 * 2 + 2], scale=1.0 / 16)
            act(g.dec[:, c2, :], g.bp[:, 127::128], AF.Exp, g.bbp, g.bdec, scale=-1.0 / 16)

        def lin_fm_gen(w, pieces, nchunk, rhs_fn, evac, nkb=2):
            for kb in range(nkb):
                sl, bf = wget(w, kb * 8, pieces)
                for oc in range(nchunk):
                    for k in range(8):
                        kk = kb * 8 + k
                        r_ap, r_bufs = rhs_fn(kk)
                        mm(psM[oc][:, :], sl[:, k, oc * 128:(oc + 1) * 128], r_ap, kk == 0, kk == nkb * 8 - 1,
                           reads=bf[k:k + 1] + r_bufs, writes=[B_psM[oc]],
                           sig=(kk == nkb * 8 - 1) or (k == 7 and oc == nchunk - 1))
                if kb == nkb - 1:
                    for oc in range(nchunk):
                        evac(oc, psM[oc], B_psM[oc])
                yield

        def lin_tm_gen(w, c0, ncols, lhs_fn, evac, nkb=2, nsub=4):
            for kb in range(nkb):
                sl, bf = wget(w, kb * 8, [(c0, ncols)])
                for sub in range(nsub):
                    for k in range(8):
                        kk = kb * 8 + k
                        l_ap, l_bufs = lhs_fn(kk, sub)
                        mm(psM[sub][:, 0:ncols], l_ap, sl[:, k, 0:ncols], kk == 0, kk == nkb * 8 - 1,
                           reads=l_bufs + bf[k:k + 1], writes=[B_psM[sub]],
                           sig=(kk == nkb * 8 - 1) or (k == 7 and sub == nsub - 1))
                if kb == nkb - 1:
                    for sub in range(nsub):
                        evac(sub, psM[sub], B_psM[sub])
                yield

        def gla_stage1(h_, g):
            pieces = [(OFF_GQ + h_ * 256, 256), (OFF_GK + h_ * 256, 256)]
            if h_ != 0:
                for c2 in range(2):
                    gla_decay(g, h_, c2)

            def evac_qk(oc, ps, bps):
                if oc < 2:
                    c2 = oc
                    stt(g.qinT[:, c2, :], ps[:, :], 1.0 / 16, g.E1[:, c2, :], ALU.mult, ALU.mult,
                        [bps] + g.bE1[c2 * 2:c2 * 2 + 2], g.bqin[c2:c2 + 1])
                else:
                    c2 = oc - 2
                    tt(g.kinT[:, c2, :], ps[:, :], g.E2[:, c2, :], ALU.mult, [bps] + g.bE2[c2 * 2:c2 * 2 + 2], g.bkin[c2:c2 + 1])
                    for ch in range(4):
                        stt(g.kdecT[:, c2, ch * 128:(ch + 1) * 128], g.E2[:, c2, ch * 128:(ch + 1) * 128], g.dec[:, c2, ch:ch + 1],
                            ps[:, ch * 128:(ch + 1) * 128], ALU.mult, ALU.mult,
                            [bps] + g.bE2[c2 * 2:c2 * 2 + 2] + g.bdec, g.bkdecT[c2:c2 + 1])

            yield from lin_fm_gen(w_in, pieces, 4, xT_rhs, evac_qk)

            def evac_v(sub, ps, bps):
                cp_rr(g.v_tm[:, sub, :], ps[:, :], [bps], g.bvtm[sub:sub + 1])

            yield from lin_tm_gen(w_in, OFF_GV + h_ * 512, 512, xT_lhs, evac_v)

            def evac_g(sub, ps, bps):
                j = sub % 2
                act(rtmp[:, j, :], ps[:, :], AF.Silu, [bps], [B_rtmp[j]])
                tt(g.gsw[:, sub, :], rtmp[:, j, :], gw_bc[:, :], ALU.mult, [B_rtmp[j], B_gw], g.bgsw[sub:sub + 1])

            yield from lin_tm_gen(w_in, OFF_GO + h_ * 512, 512, xT_lhs, evac_g)
            for c2 in range(2):
                for sub in range(4):
                    j = c2 * 4 + sub
                    tr(psT[:, j * 128:(j + 1) * 128], g.kdecT[:, c2, sub * 128:(sub + 1) * 128], ident_b[:],
                       reads=g.bkdecT[c2:c2 + 1] + [B_ident_b], writes=[B_psT], sig=(j == 7))
            cp("act", g.kdec_tm.rearrange("p a b c -> p (a b c)"), psT[:, :], [B_psT], g.bkdtm)
            yield

        def gla_stage2(h_, g):
            def ph_a(ch):
                cs = slice(ch * 128, (ch + 1) * 128)
                for c2 in range(2):
                    mm(psA[0][:, 0:128], g.kinT[:, c2, cs], g.qinT[:, c2, cs], c2 == 0, c2 == 1,
                       reads=g.bkin[c2:c2 + 1] + g.bqin[c2:c2 + 1], writes=[B_psA[0]], sig=(c2 == 1))
                tt(g.sT, psA[0][:, 0:128], cmask[:, :], ALU.mult, [B_psA[0], B_cmask], g.bsT)

            def ph_b(ch):
                cs = slice(ch * 128, (ch + 1) * 128)
                for c2 in range(2):
                    mm(psA[1][:, :], g.qinT[:, c2, cs], S_b[:, h_ * 2 + c2, :], c2 == 0, False,
                       reads=g.bqin[c2:c2 + 1] + [B_S_b[h_ * 2 + c2]], writes=[B_psA[1]], sig=False)
                mm(psA[1][:, :], g.sT, g.v_tm[:, ch, :], False, True,
                   reads=g.bsT + g.bvtm[ch:ch + 1], writes=[B_psA[1]], sig=True)
                for c2 in range(2):
                    gi = h_ * 2 + c2
                    ub, bub = psA[2], B_psA[2]
                    mm(ub[:, :], g.kdec_tm[:, c2, ch, :], g.v_tm[:, ch, :], True, True,
                       reads=g.bkdtm + g.bvtm[ch:ch + 1], writes=[bub], sig=True)
                    stt(S_f[:, gi, :], S_f[:, gi, :], g.dec[:, c2, ch:ch + 1], ub[:, :], ALU.mult, ALU.add,
                        [B_S_f[gi], bub] + g.bdec, [B_S_f[gi]])
                    cp("act", S_b[:, gi, :], S_f[:, gi, :], [B_S_f[gi]], [B_S_b[gi]])
                act(g.o_g, psA[1][:, :], AF.Square, [B_psA[1]], g.bog + [b_ssq], accum=ssq)
                act(rst, ssq, AF.Ln, [b_ssq], [b_rst], scale=1.0 / 512, bias=RMS_EPS)
                act(rst, rst, AF.Exp, [b_rst], [b_rst], scale=-0.5)
                stt(g.o_g, psA[1][:, :], rst, g.gsw[:, ch, :], ALU.mult, ALU.mult,
                    [B_psA[1], b_rst] + g.bgsw[ch:ch + 1], g.bog)

            def ph_c(ch):
                cs = slice(ch * 128, (ch + 1) * 128)
                for j in range(4):
                    tr(psT[:, j * 128:(j + 1) * 128], g.o_g[:, j * 128:(j + 1) * 128], ident_b[:],
                       reads=g.bog + [B_ident_b], writes=[B_psT], sig=(j == 3))
                cp("dve", o_aT[:, h_ * 4:(h_ + 1) * 4, cs], psT[:, 0:512].rearrange("p (a b) -> p a b", a=4),
                   [B_psT], bo_aT[h_ * 4:(h_ + 1) * 4])

            for step in range(6):
                if 0 <= step - 2 < 4:
                    ph_c(step - 2)
                if 0 <= step - 1 < 4:
                    ph_b(step - 1)
                if step < 4:
                    ph_a(step)
                yield

        def exhaust(gen):
            for _ in gen:
                pass

        def interleave(ga, gb_, nb=1):
            da = db = False
            while not (da and db):
                if not da:
                    try:
                        next(ga)
                    except StopIteration:
                        da = True
                for _ in range(nb):
                    if not db:
                        try:
                            next(gb_)
                        except StopIteration:
                            db = True

        def gla_pre():
            gla_lr()
            for c2 in range(2):
                gla_decay(GSETS[1], 0, c2)

        def gla_tile(pending=None):
            if pending is None:
                exhaust(gla_stage1(0, GSETS[1]))
            else:
                interleave(gla_stage1(0, GSETS[1]), pending, nb=2)
            for h_ in range(1, 4):
                interleave(gla_stage1(h_, GSETS[(h_ + 1) % 2]), gla_stage2(h_ - 1, GSETS[h_ % 2]))
            return gla_stage2(3, GSETS[0])

        B_wk1l, B_wv1l = Buf("wk1l"), Buf("wv1l")
        o_aT, bo_aT = RB.view(0, [16, 512], BF16)
        o_bT, bo_bT = RB.view(16384, [16, 512], BF16)
        mgT, bmgT = RB.view(32768, [16, 512], BF16)
        t1, bt1 = RB.view(49152, [4, 512], F32)
        t2, bt2 = RB.view(57344, [512], F32)
        hT, bhT = RB.view(0, [64, 512], BF16)

        if NT_PRE > 0:
            wk1, bwk1 = RB.view(8 * 8192, [16, 1024], BF16)
            Mv, bM = RB.view(0, [8, 2048], F32)

            class PV:
                pass

            pa_ = PV()
            pa_.E2p, bE2pA = RX.view(0, [2, 512], F32)
            pa_.bE2 = lambda c2, b=bE2pA: b[c2 * 2:c2 * 2 + 2]
            pa_.bpp, bbppA = RX.view(6144, [2, 512], F32)
            pa_.bbp = lambda c2, b=bbppA: b[c2 * 2:c2 * 2 + 2]
            pa_.bbp_all = bbppA
            pa_.kdecTp, bkdA = RX.view(10240, [2, 512], BF16)
            pa_.bkdT = lambda c2, b=bkdA: b[c2:c2 + 1]
            pa_.kdtmp, pa_.bkdtm = RX.view(12288, [2, 4, 128], BF16)
            pa_.decp, pa_.bdecp = RX.view(14336, [2, 4], F32)
            pb_ = PV()
            pb_.E2p = S_f[:, 0:2, :]
            pb_.bE2 = lambda c2: [B_S_f[c2]]
            pb_.bpp = S_f[:, 2:4, :]
            pb_.bbp = lambda c2: [B_S_f[2 + c2]]
            pb_.bbp_all = [B_S_f[2], B_S_f[3]]
            pb_.kdecTp = S_f[:, 4, :].bitcast(BF16).rearrange("p (a b) -> p a b", a=2)[:, :, 0:512]
            pb_.bkdT = lambda c2: [B_S_f[4]]
            pb_.kdtmp = S_f[:, 5, :].bitcast(BF16).rearrange("p (a b c) -> p a b c", a=2, b=4)
            pb_.bkdtm = [B_S_f[5]]
            pb_.decp = S_f[:, 6, 0:8].rearrange("p (a b) -> p a b", a=2)
            pb_.bdecp = [B_S_f[6]]
            PSETS = [pa_, pb_]
            lap, blap = RX.view(4096, [512], F32)
            xbfB, bxbfB = RX.view(16384, [4, 2048], BF16)
            xbfA = gb[:, :, :].rearrange("p a b -> p (a b)").bitcast(BF16).rearrange("p (s d) -> p s d", s=4)
            xbfs = [(xbfA, [B_gb], Buf("xbfA_l")), (xbfB, bxbfB, Buf("xbfB_l"))]
            Rt, b_Rt = smallv("Rt", 24, 8)
            blv, b_blv = smallv("blv", 32, 8)
            exv, b_exv = smallv("exv", 40, 8)
            blv3 = blv.rearrange("p (a b) -> p a b", a=2)
            exv3 = exv.rearrange("p (a b) -> p a b", a=2)
            fns = []
            for kq in range(4):
                fns.append(lambda h, kq=kq: h.dma_start(
                    out=wk1[:, kq * 4:(kq + 1) * 4, :],
                    in_=w_in[kq * 512:(kq + 1) * 512, OFF_GK:OFF_GK + 1024].rearrange("(k p) c -> p k c", p=128)))
            P.dma("pool", fns, B_wk1l, writes=bwk1)
            for gc in range(8):
                P.op("dve", lambda h, gc=gc: h.memset(Mv[:, gc, :], 0.0), writes=bM[gc * 8:(gc + 1) * 8])
            P.op("dve", lambda h: h.memset(Rt, 0.0), writes=[b_Rt])
            grp = [0]

            def pre_prologue(tp):
                t = NT_PRE - 1 - tp
                xbf, bxbf, bxl = xbfs[tp % 2]
                P.dma("pool", [lambda h, s_=s_, xbf=xbf, t=t: h.dma_start(out=xbf[:, s_, :], in_=xprev[t * T + s_ * 128:t * T + (s_ + 1) * 128, :])
                               for s_ in range(4)], bxl, writes=bxbf)
                banks = [(psT[:, :], B_psT), (psA_bf[2], B_psA[2])]
                for cp2 in range(8):
                    bk, bbk = banks[cp2 % 2]
                    for cc in range(2):
                        c = cp2 * 2 + cc
                        for s_ in range(4):
                            tr(bk[:, cc * 512 + s_ * 128: cc * 512 + (s_ + 1) * 128], xbf[:, s_, c * 128:(c + 1) * 128], ident_b[:],
                               reads=bxbf + [B_ident_b], writes=[bbk], sig=(cc == 1 and s_ == 3))
                    cp_rr(xT[:, cp2 * 2:cp2 * 2 + 2, :], bk.rearrange("p (a b) -> p a b", a=2), [bbk],
                          [B_xT[cp2 * 2], B_xT[cp2 * 2 + 1]])
                gla_lr()

            def pre_stage1(tp, h_, v):
                for c2 in range(2):
                    gc = h_ * 2 + c2
                    zb, bzb = (psA[2][:, :], B_psA[2]) if c2 == 0 else (psT_f, B_psT)
                    mm(zb, w_au[0:16, gc * 128:(gc + 1) * 128], lrT[0:16, :], True, True,
                       reads=[B_wau, B_lrT], writes=[bzb], sig=True)
                    act(lap, zb, AF.Exp, [bzb, B_nba], blap, scale=-1.0, bias=nba[:, gc:gc + 1])
                    act(lap, lap, AF.Ln, blap, blap, bias=1.0)
                    for ch in range(4):
                        P.op("dve", lambda h, ch=ch, c2=c2: h.tensor_tensor_scan(
                            out=v.bpp[:, c2, ch * 128:(ch + 1) * 128], data0=ones_f[:, :],
                            data1=lap[:, ch * 128:(ch + 1) * 128], initial=0.0, op0=ALU.mult, op1=ALU.add),
                            reads=blap + [B_ones], writes=v.bbp(c2))
                ts(blv3, v.bpp[:, :, 127::128], -1.0 / 16, None, ALU.mult, None, v.bbp_all, [b_blv])
                Rh = Rt[:, h_ * 2:h_ * 2 + 2].unsqueeze(2)
                cp("dve", exv3[:, :, 3:4], Rh, [b_Rt], [b_exv])
                for ch in (2, 1, 0):
                    tt(exv3[:, :, ch:ch + 1], exv3[:, :, ch + 1:ch + 2], blv3[:, :, ch + 1:ch + 2], ALU.add, [b_exv, b_blv], [b_exv])
                tt(Rh, exv3[:, :, 0:1], blv3[:, :, 0:1], ALU.add, [b_exv, b_blv], [b_Rt])
                tt(exv, exv, blv, ALU.add, [b_exv, b_blv], [b_exv])
                for c2 in range(2):
                    for ch in range(4):
                        act(v.E2p[:, c2, ch * 128:(ch + 1) * 128], v.bpp[:, c2, ch * 128:(ch + 1) * 128], AF.Exp,
                            v.bbp(c2) + [b_exv], v.bE2(c2), scale=1.0 / 16, bias=exv3[:, c2, ch:ch + 1])

                def evac_kp(oc, ps, bps):
                    tt(v.kdecTp[:, oc, :], ps[:, :], v.E2p[:, oc, :], ALU.mult, [bps] + v.bE2(oc), v.bkdT(oc))

                lin_fm(w_in, None, 2, xT_rhs, evac_kp,
                       resident=lambda kk, oc, h_=h_: (wk1[:, kk, h_ * 256 + oc * 128: h_ * 256 + (oc + 1) * 128], bwk1[kk * 2:kk * 2 + 2]))

            def pre_stage2a(tp, h_, v):
                for c2 in range(2):
                    for sub in range(4):
                        j = c2 * 4 + sub
                        tr(psT[:, j * 128:(j + 1) * 128], v.kdecTp[:, c2, sub * 128:(sub + 1) * 128], ident_b[:],
                           reads=v.bkdT(c2) + [B_ident_b], writes=[B_psT], sig=(j == 7))
                cp("act", v.kdtmp.rearrange("p a b c -> p (a b c)"), psT[:, :], [B_psT], v.bkdtm)

            def pre_stage2b(tp, h_, v):
                xbf, bxbf, bxl = xbfs[tp % 2]
                for c2 in range(2):
                    gc = h_ * 2 + c2
                    for dpair in range(2):
                        if grp[0] % 2 == 0:
                            bk, bkb = [psM[2], psM[3]], [B_psM[2], B_psM[3]]
                        else:
                            bk, bkb = [psA[0], psA[1]], [B_psA[0], B_psA[1]]
                        grp[0] += 1
                        for db in range(2):
                            col0 = (dpair * 2 + db) * 512
                            for sub in range(4):
                                mm(bk[db][:, :], v.kdtmp[:, c2, sub, :], xbf[:, sub, col0:col0 + 512], sub == 0, sub == 3,
                                   reads=v.bkdtm + bxbf, writes=[bkb[db]], sig=(sub == 3))
                        for db in range(2):
                            col0 = (dpair * 2 + db) * 512
                            mb = bM[gc * 8 + (dpair * 2 + db) * 2: gc * 8 + (dpair * 2 + db) * 2 + 2]
                            tt(Mv[:, gc, col0:col0 + 512], Mv[:, gc, col0:col0 + 512], bk[db][:, :], ALU.add,
                               [bkb[db]] + mb, mb)

            items = [(tp, h_) for tp in range(NT_PRE) for h_ in range(4)]
            for i in range(len(items) + 1):
                if i < len(items):
                    tp, h_ = items[i]
                    if h_ == 0:
                        pre_prologue(tp)
                    pre_stage1(tp, h_, PSETS[i % 2])
                if i >= 1:
                    tp, h_ = items[i - 1]
                    pre_stage2b(tp, h_, PSETS[(i - 1) % 2])
                if i < len(items):
                    tp, h_ = items[i]
                    pre_stage2a(tp, h_, PSETS[i % 2])
            MT, bMT = RX.view(0, [16, 1024], BF16)
            q4 = 0
            for gc in range(8):
                for d4 in range(4):
                    pa, bpa = psA[q4 % 2], B_psA[q4 % 2]
                    q4 += 1
                    for j in range(4):
                        dc = d4 * 4 + j
                        tr(pa[:, j * 128:(j + 1) * 128], Mv[:, gc, dc * 128:(dc + 1) * 128], ident_f[:],
                           reads=bM[gc * 8 + dc // 2: gc * 8 + dc // 2 + 1] + [B_ident_f], writes=[bpa], sig=(j == 3))
                    cp_rr(MT[:, d4 * 4:(d4 + 1) * 4, gc * 128:(gc + 1) * 128], pa[:, :].rearrange("p (a b) -> p a b", a=4),
                          [bpa], bMT[d4 * 8:(d4 + 1) * 8])
            for h_ in range(4):
                def evac_s0(sub, ps, bps, h_=h_):
                    gi = h_ * 2 + sub
                    cp("dve", S_f[:, gi, :], ps[:, :], [bps], [B_S_f[gi]])
                    cp("act", S_b[:, gi, :], S_f[:, gi, :], [B_S_f[gi]], [B_S_b[gi]])

                lin_tm(w_in, OFF_GV + h_ * 512, 512,
                       lambda k, sub, h_=h_: (MT[:, k, (h_ * 2 + sub) * 128:(h_ * 2 + sub + 1) * 128], bMT[k * 2:k * 2 + 2]),
                       evac_s0, nsub=2)

        qT_s2 = [RB.view(32768, [4, 512], BF16), RB.view(36864, [4, 512], BF16)]
        kT2, bkT2 = RB.view(40960, [4, 640], BF16)
        v_aug, bvaug = RB.view(46080, [5, 4, 65], BF16)
        msk2 = [RB.view(49152, [2048], BF16), RB.view(53248, [2048], BF16)]
        pT2 = [RB.view(57344, [1024], BF16), RB.view(59392, [1024], BF16)]
        obt2 = [RB.view(61440, [256], BF16), RB.view(62464, [256], BF16)]
        rden, b_rden = smallv("rden", 8, 8)
        B_mskl = [Buf("mskl0"), Buf("mskl1")]
        k_pieces = []
        for kv in range(4):
            k_pieces += [(OFF_SK + kv * 64, 64), (OFF_SK + kv * 64, 64)]

        def swa_kv_proj(ntok, k_dst, k_dst_bufs, v_dst_fn):
            def rhs(k):
                return xT[:, k, 0:ntok], [B_xT[k]]

            def evac_k(oc, ps, bps):
                cp_rr(k_dst(oc), ps[:, 0:ntok], [bps], k_dst_bufs)

            lin_fm(w_in, k_pieces, 4, rhs, evac_k)

            def evac_v(sub, ps, bps):
                o_, ob_ = v_dst_fn(sub)
                cp_rr(o_, ps[:, 0:256].rearrange("p (a b) -> p a b", a=4), [bps], ob_)

            lin_tm(w_in, OFF_SV, 256, xT_lhs, evac_v, nsub=ntok // 128)

        load_x_tile(xhalo, 128)
        transpose_to_xT(ns=1)
        P.op("dve", lambda h: h.memset(v_carry[:, :, 64:65], 1.0), writes=[B_vc])
        swa_kv_proj(128, lambda oc: kT_carry[:, oc, :], [B_kc], lambda sub: (v_carry[:, :, 0:64], [B_vc]))

        sgA, bsgA = RX.view(0, [16, 512], BF16)
        sgB, bsgB = RX.view(16384, [16, 512], BF16)
        mv, b_mv = smallv("mv", 16, 2)
        lrs, b_lrs = smallv("lrs", 18)
        nmr, b_nmr = smallv("nmr", 19)

        mv4, b_mv4 = smallv("mv4", 48, 8)
        lrs4, b_lrs4 = smallv("lrs4", 56, 4)
        nmr4, b_nmr4 = smallv("nmr4", 60, 4)
        stats4 = st.enter_context(nc.sbuf_tensor("stats4", [128, 4, 4, 6], F32))
        B_stats4 = [Buf(f"stats4_{i}") for i in range(4)]

        def ln_stats(sub, q):
            xb = xs_bufs[sub * 8:(sub + 1) * 8]
            P.op("dve", lambda h: h.bn_stats(out=stats4[:, sub, q, :], in_=xs_ap[:, sub, q * 512:(q + 1) * 512]),
                 reads=xb[q * 2:q * 2 + 2], writes=[B_stats4[sub]])

        def layer_norm_gen(g_ap, b_ap, defer_affine=False, stats_done=False):
            xbs = [xs_bufs[sub * 8:(sub + 1) * 8] for sub in range(4)]
            for sub in range(4):
                if not stats_done:
                    for q in range(4):
                        ln_stats(sub, q)
                P.op("dve", lambda h, sub=sub: h.bn_aggr(out=mv4[:, sub * 2:sub * 2 + 2],
                                                          in_=stats4[:, sub, :, :].rearrange("p a b -> p (a b)")),
                     reads=[B_stats4[sub]], writes=[b_mv4])
                if sub % 2 == 1:
                    yield
            act(lrs4, mv4[:, 1::2], AF.Ln, [b_mv4], [b_lrs4], bias=LN_EPS)
            act(lrs4, lrs4, AF.Exp, [b_lrs4], [b_lrs4], scale=-0.5)
            stt(nmr4, mv4[:, 0::2], -1.0, lrs4, ALU.mult, ALU.mult, [b_mv4, b_lrs4], [b_nmr4])
            for sub in range(4):
                if sub % 2 == 0:
                    act(xs_ap[:, sub, :], xs_ap[:, sub, :], AF.Identity, xbs[sub] + [b_lrs4, b_nmr4], xbs[sub],
                        scale=lrs4[:, sub:sub + 1], bias=nmr4[:, sub:sub + 1])
                else:
                    ts(xs_ap[:, sub, :], xs_ap[:, sub, :], lrs4[:, sub:sub + 1], nmr4[:, sub:sub + 1], ALU.mult, ALU.add,
                       xbs[sub] + [b_lrs4, b_nmr4], xbs[sub])
            yield
            if not defer_affine:
                for sub in range(4):
                    xb = xs_bufs[sub * 8:(sub + 1) * 8]
                    tt(xs_ap[:, sub, :], xs_ap[:, sub, :], g_ap, ALU.mult, xb + [B_gb], xb)
                    tt(xs_ap[:, sub, :], xs_ap[:, sub, :], b_ap, ALU.add, xb + [B_gb], xb)
                    yield

        def layer_norm_all(g_ap, b_ap, defer_affine=False, stats_done=False):
            for _ in layer_norm_gen(g_ap, b_ap, defer_affine, stats_done):
                pass

        def ln_affine(g_ap, b_ap):
            for sub in range(4):
                xb = xs_bufs[sub * 8:(sub + 1) * 8]
                tt(xs_ap[:, sub, :], xs_ap[:, sub, :], g_ap, ALU.mult, xb + [B_gb], xb)
                tt(xs_ap[:, sub, :], xs_ap[:, sub, :], b_ap, ALU.add, xb + [B_gb], xb)

        def load_gb(g, b):
            P.dma("sp", [lambda h: h.dma_start(out=gb[:, 0, :], in_=g.partition_broadcast(128)),
                         lambda h: h.dma_start(out=gb[:, 1, :], in_=b.partition_broadcast(128))],
                  B_gbload, writes=[B_gb])

        xbf_m = gb[:, :, :].rearrange("p a b -> p (a b)").bitcast(BF16).rearrange("p (s d) -> p s d", s=4)
        B_xinl = Buf("xinload")

        def prefetch_x(t):
            src = x[t * T:(t + 1) * T, :]
            P.dma("pool", [lambda h, s_=s_: h.dma_start(out=xbf_m[:, s_, :], in_=src[s_ * 128:(s_ + 1) * 128, :]) for s_ in range(4)],
                  B_xinl, writes=[B_gb])
            banks = [(psT[:, :], B_psT), (psA_bf[0], B_psA[0]), (psA_bf[1], B_psA[1]), (psA_bf[2], B_psA[2])]
            for cp2 in range(8):
                bk, bbk = banks[cp2 % 4]
                for cc in range(2):
                    c = cp2 * 2 + cc
                    for s_ in range(4):
                        tr(bk[:, cc * 512 + s_ * 128: cc * 512 + (s_ + 1) * 128], xbf_m[:, s_, c * 128:(c + 1) * 128], ident_b[:],
                           reads=[B_gb, B_ident_b], writes=[bbk], sig=(cc == 1 and s_ == 3))
                cp_rr(xT[:, cp2 * 2:cp2 * 2 + 2, :], bk.rearrange("p (a b) -> p a b", a=2), [bbk],
                      [B_xT[cp2 * 2], B_xT[cp2 * 2 + 1]])

        prefetch_x(0)
        gla_pre()
        pending_ln2 = None

        def ln2_store_gen(t):
            gen = layer_norm_gen(gb[:, 0, :], gb[:, 1, :], defer_affine=True, stats_done=True)
            for _ in gen:
                yield
            for sub in range(4):
                xb = xs_bufs[sub * 8:(sub + 1) * 8]
                tt(xs_ap[:, sub, :], xs_ap[:, sub, :], gb[:, 0, :], ALU.mult, xb + [B_gb], xb)
                tt(xs_ap[:, sub, :], xs_ap[:, sub, :], gb[:, 1, :], ALU.add, xb + [B_gb], xb)
                P.dma("sp", [lambda h, sub=sub: h.dma_start(out=out[t * T + sub * 128:t * T + (sub + 1) * 128, :], in_=xs_ap[:, sub, :])],
                      B_xstore4[sub], reads=xb)
                yield

        for t in range(NT_MAIN):
            xt_src = x[t * T:(t + 1) * T, :]
            st2_3 = gla_tile(pending_ln2)
            pending_ln2 = None

            def swa_kvproj_gen():
                P.op("dve", lambda h: h.memset(v_aug[:, :, :, 64:65], 1.0), writes=bvaug)

                def evac_k(oc, ps, bps):
                    cp_rr(kT2[:, oc, 128:640], ps[:, :], [bps], bkT2)

                yield from lin_fm_gen(w_in, k_pieces, 4, xT_rhs, evac_k)

                def evac_v(sub, ps, bps):
                    cp_rr(v_aug[:, 1 + sub, :, 0:64], ps[:, 0:256].rearrange("p (a b) -> p a b", a=4), [bps], bvaug)

                yield from lin_tm_gen(w_in, OFF_SV, 256, xT_lhs, evac_v)

            def swa_qproj(kv):
                mskv, bmskv = msk2[kv % 2]
                P.dma("sp", [lambda h: h.dma_start(out=mskv, in_=swa_mask[:, kv, :])], B_mskl[kv % 2], writes=bmskv)
                qd, bqd = qT_s2[kv % 2]
                for kb in range(2):
                    sl, bf = wget(w_in, kb * 8, [(OFF_SQ + kv * 512, 512)])
                    for oc in range(4):
                        for k in range(8):
                            kk = kb * 8 + k
                            mm(psM[oc][:, :], sl[:, k, oc * 128:(oc + 1) * 128], xT[:, kk, :], kk == 0, kk == 15,
                               reads=bf[k:k + 1] + [B_xT[kk]], writes=[B_psM[oc]], sig=(kk == 15) or (k == 7 and oc == 3))
                        if kb == 1:
                            cp_rr(qd[:, oc, :], psM[oc][:, :], [B_psM[oc]], bqd[oc:oc + 1])
                        yield

            def swa_attn(kv, t=t):
                qd, bqd = qT_s2[kv % 2]
                mskv, bmskv = msk2[kv % 2]
                items = [(n, hs) for n in range(4) for hs in range(2)]

                def ph_a(i):
                    n, hs = items[i]
                    pT, bpT = pT2[i % 2]
                    qs = slice(n * 128, (n + 1) * 128)
                    for par in range(2):
                        rows = slice(par * 64, (par + 1) * 64)
                        for b_ in range(2):
                            for cc in range(2):
                                c = 2 * hs + cc
                                if n == 0 and b_ == 0:
                                    l_ap, l_b = kT_carry[rows, kv, :], [B_kc]
                                else:
                                    kb0 = (n + b_) * 128
                                    l_ap, l_b = kT2[rows, kv, kb0:kb0 + 128], bkT2
                                col = (b_ * 2 + cc) * 128
                                mm(psA[par][:, col:col + 128], l_ap, qd[rows, c, qs], True, True,
                                   reads=l_b + bqd[c:c + 1], writes=[B_psA[par]], sig=(b_ == 1 and cc == 1))
                    for par in range(2):
                        act(pT[:, par * 512:(par + 1) * 512], psA[par][:, :], AF.Exp, [B_psA[par]], bpT[par:par + 1], scale=0.125)
                    tt(pT, pT, mskv[:, hs * 1024:(hs + 1) * 1024], ALU.mult, bpT + bmskv[hs * 2:hs * 2 + 2], bpT)
                    if t == 0 and n == 0:
                        pv_ = pT.rearrange("p (a b c) -> p a b c", a=2, b=2)[:, :, 0, :]
                        ts(pv_, pv_, hval[:, 0:1], None, ALU.mult, None, bpT + [B_hval], bpT)

                def ph_b(i):
                    n, hs = items[i]
                    pT, bpT = pT2[i % 2]
                    ob, bob = obt2[i % 2]
                    if n == 0:
                        vp, vpb = v_carry[:, kv, :], [B_vc]
                    else:
                        vp, vpb = v_aug[:, n, kv, :], bvaug
                    for lg in range(4):
                        cc, par = lg // 2, lg % 2
                        c0 = par * 512 + cc * 128
                        mm(psA[2][:, lg * 65:(lg + 1) * 65], pT[:, c0:c0 + 128], vp, True, False,
                           reads=bpT + vpb, writes=[B_psA[2]], sig=False)
                        mm(psA[2][:, lg * 65:(lg + 1) * 65], pT[:, c0 + 256:c0 + 384], v_aug[:, n + 1, kv, :], False, True,
                           reads=bpT + bvaug, writes=[B_psA[2]], sig=(lg == 3))
                    pv = psA[2][:, 0:260].rearrange("p (a b) -> p a b", a=4)
                    g0 = kv * 8 + hs * 4
                    tt(rden[:, 0:4], pv[:, :, 64:65].rearrange("p a b -> p (a b)"), sinkexp[:, g0:g0 + 4], ALU.add,
                       [B_psA[2], B_sink], [b_rden])
                    P.op("dve", lambda h: h.reciprocal(out=rden[:, 0:4], in_=rden[:, 0:4]), reads=[b_rden], writes=[b_rden])
                    tt(ob.rearrange("p (a b) -> p a b", a=4), pv[:, :, 0:64],
                       rden[:, 0:4].unsqueeze(2).to_broadcast([128, 4, 64]), ALU.mult, [B_psA[2], b_rden], bob)

                def ph_c(i):
                    n, hs = items[i]
                    ob, bob = obt2[i % 2]
                    qs = slice(n * 128, (n + 1) * 128)
                    for j in range(2):
                        tr(psT[:, j * 128:(j + 1) * 128], ob[:, j * 128:(j + 1) * 128], ident_b[:],
                           reads=bob + [B_ident_b], writes=[B_psT], sig=(j == 1))
                    c0 = kv * 4 + hs * 2
                    cp("act", o_bT[:, c0:c0 + 2, qs], psT[:, 0:256].rearrange("p (a b) -> p a b", a=2),
                       [B_psT], bo_bT[c0:c0 + 2])

                for step in range(len(items) + 2):
                    if 0 <= step - 2 < len(items):
                        ph_c(step - 2)
                    if 0 <= step - 1 < len(items):
                        ph_b(step - 1)
                    if step < len(items):
                        ph_a(step)
                    yield

            def gate_gen(off, sg, bsg, cb):
                def evac_s(oc, ps, bps):
                    c = cb * 4 + oc
                    act(sg[:, c, :], ps[:, :], AF.Sigmoid, [bps], bsg[c:c + 1])

                yield from lin_fm_gen(w_in, [(off + cb * 512, 512)], 4, xT_rhs, evac_s)

            done = set()
            heavy = [("kvproj", swa_kvproj_gen(), []), ("q0", swa_qproj(0), []), ("q1", swa_qproj(1), []),
                     ("gA0", gate_gen(OFF_GA, sgA, bsgA, 0), ["st2_3"]), ("gA1", gate_gen(OFF_GA, sgA, bsgA, 1), ["st2_3"]),
                     ("q2", swa_qproj(2), ["attn0"]),
                     ("gA2", gate_gen(OFF_GA, sgA, bsgA, 2), ["st2_3"]), ("gA3", gate_gen(OFF_GA, sgA, bsgA, 3), ["st2_3"]),
                     ("q3", swa_qproj(3), ["attn1"])]
            heavy += [("gB%d" % cb, gate_gen(OFF_GB, sgB, bsgB, cb), ["st2_3"]) for cb in range(4)]
            light = [("st2_3", st2_3, []), ("attn0", swa_attn(0), ["kvproj", "q0"]), ("attn1", swa_attn(1), ["q1"]),
                     ("attn2", swa_attn(2), ["q2"]), ("attn3", swa_attn(3), ["q3"])]

            def step(q):
                if not q:
                    return False
                name, gen, req = q[0]
                if any(r not in done for r in req):
                    return False
                try:
                    next(gen)
                except StopIteration:
                    done.add(name)
                    q.pop(0)
                return True

            while heavy or light:
                a_ = step(heavy)
                b_ = step(light)
                assert a_ or b_, ("scheduling deadlock", heavy[:1], light[:1])
            cp("act", kT_carry[:, :, :], kT2[:, :, 512:640], bkT2, [B_kc])
            cp("dve", v_carry[:, :, :], v_aug[:, 4, :, :], bvaug, [B_vc])

            for cb in range(4):
                def evac_a(oc, ps, bps, cb=cb):
                    c = cb * 4 + oc
                    tt(t1[:, oc, :], ps[:, :], sgA[:, c, :], ALU.mult, [bps] + bsgA[c:c + 1], bt1[oc * 2:oc * 2 + 2])

                lin_fm(w_bg, [(cb * 512, 512)], 4,
                       lambda k: (o_aT[:, k, :], bo_aT[k:k + 1]), evac_a)

                def evac_b(oc, ps, bps, cb=cb):
                    c = cb * 4 + oc
                    tt(t2, ps[:, :], sgB[:, c, :], ALU.mult, [bps] + bsgB[c:c + 1], bt2)
                    tt(mgT[:, c, :], t2, t1[:, oc, :], ALU.add, bt2 + bt1[oc * 2:oc * 2 + 2], bmgT[c:c + 1])

                lin_fm(w_bs, [(cb * 512, 512)], 4,
                       lambda k: (o_bT[:, k, :], bo_bT[k:k + 1]), evac_b)

            load_x_tile(xt_src)
            load_gb(ln1_g, ln1_b)
            for cb in range(4):
                def evac_o(sub, ps, bps, cb=cb):
                    xb = xs_bufs[sub * 8 + cb * 2: sub * 8 + cb * 2 + 2]
                    stt(xs_ap[:, sub, cb * 512:(cb + 1) * 512], xs_ap[:, sub, cb * 512:(cb + 1) * 512], ALPHA, ps[:, :],
                        ALU.mult, ALU.add, [bps] + xb, xb)
                    ln_stats(sub, cb)

                lin_tm(w_out, cb * 512, 512,
                       lambda k, sub: (mgT[:, k, sub * 128:(sub + 1) * 128], bmgT[k:k + 1]), evac_o)
            layer_norm_all(gb[:, 0, :], gb[:, 1, :], defer_affine=True, stats_done=True)
            transpose_to_xT(affine=True)

            for cb in range(16):
                def evac_h(oc, ps, bps, cb=cb):
                    m = cb * 4 + oc
                    j = m % 2
                    act(rtmp[:, j, :], ps[:, :], AF.Relu, [bps], [B_rtmp[j]])
                    tt(hT[:, m, :], rtmp[:, j, :], rtmp[:, j, :], ALU.mult, [B_rtmp[j]], bhT[m:m + 1])

                lin_fm(w_up, [(cb * 512, 512)], 4, xT_rhs, evac_h)
                if cb == 2:
                    ln_affine(gb[:, 0, :], gb[:, 1, :])
            for cb in range(4):
                def evac_d(sub, ps, bps, cb=cb):
                    xb = xs_bufs[sub * 8 + cb * 2: sub * 8 + cb * 2 + 2]
                    stt(xs_ap[:, sub, cb * 512:(cb + 1) * 512], xs_ap[:, sub, cb * 512:(cb + 1) * 512], ALPHA, ps[:, :],
                        ALU.mult, ALU.add, [bps] + xb, xb)
                    ln_stats(sub, cb)

                lin_tm(w_dn, cb * 512, 512,
                       lambda k, sub: (hT[:, k, sub * 128:(sub + 1) * 128], bhT[k:k + 1]), evac_d, nkb=8)
                if cb == 0:
                    if t + 1 < NT_MAIN:
                        prefetch_x(t + 1)
                    load_gb(ln2_g, ln2_b)
            if t + 1 < NT_MAIN:
                gla_pre()
            pending_ln2 = ln2_store_gen(t)
            if t + 1 == NT_MAIN:
                exhaust(pending_ln2)

        P.final_wait("sp", xs_bufs)
        with nc.Block() as block:
            P.emit(block)
    return nc


def make_swa_mask():
    s = np.arange(128, dtype=np.float64)[:, None]
    q = np.arange(128, dtype=np.float64)[None, :]
    m = np.zeros((128, 4, 2, 2, 2, 2, 128), dtype=np.float64)
    for kv in range(4):
        for g in range(8):
            hh = kv * 8 + g
            slope = 2.0 ** (-8.0 * (hh + 1) / 32)
            c, par = g // 2, g % 2
            hs, cc = c // 2, c % 2
            m[:, kv, hs, par, 0, cc, :] = np.where(s > q, np.exp(-slope * (q - s + 128)), 0.0)
            m[:, kv, hs, par, 1, cc, :] = np.where(s <= q, np.exp(-slope * (q - s)), 0.0)
    return m.reshape(128, 4, 2048).astype(ml_dtypes.bfloat16)


_NC_CACHE = {}


def kernel(x, w_in, w_alpha_up, b_alpha, gla_norm_w, attn_sinks, w_branch_gla, w_branch_swa, w_out,
           ln1_g, ln1_b, w_ff_up, w_ff_down, ln2_g, ln2_b):
    x = np.ascontiguousarray(np.asarray(x, dtype=np.float32))
    B, S, _ = x.shape
    NCORE = 8
    per = NCORE // B
    SEG = S // per
    NT_MAIN = SEG // T
    NT_PRE = (per - 1) * NT_MAIN
    key = (NT_MAIN, NT_PRE)
    if key not in _NC_CACHE:
        _NC_CACHE[key] = build(NT_MAIN, NT_PRE)
    nc = _NC_CACHE[key]
    mask = make_swa_mask()
    shared = {
        "w_in": np.ascontiguousarray(np.asarray(w_in, np.float32)),
        "w_alpha_up": np.ascontiguousarray(np.asarray(w_alpha_up, np.float32)),
        "b_alpha": np.asarray(b_alpha, np.float32), "gla_norm_w": np.asarray(gla_norm_w, np.float32),
        "attn_sinks": np.asarray(attn_sinks, np.float32),
        "w_branch_gla": np.ascontiguousarray(np.asarray(w_branch_gla, np.float32)),
        "w_branch_swa": np.ascontiguousarray(np.asarray(w_branch_swa, np.float32)),
        "w_out": np.ascontiguousarray(np.asarray(w_out, np.float32)),
        "ln1_g": np.asarray(ln1_g, np.float32), "ln1_b": np.asarray(ln1_b, np.float32),
        "w_ff_up": np.ascontiguousarray(np.asarray(w_ff_up, np.float32)),
        "w_ff_down": np.ascontiguousarray(np.asarray(w_ff_down, np.float32)),
        "ln2_g": np.asarray(ln2_g, np.float32), "ln2_b": np.asarray(ln2_b, np.float32),
        "swa_mask": mask,
    }
    in_maps = []
    for c in range(NCORE):
        b, j = c // per, c % per
        m = dict(shared)
        m["x"] = x[b, j * SEG:(j + 1) * SEG]
        xp = np.zeros((max(NT_PRE, 1) * T, D), np.float32)
        if j > 0 and NT_PRE > 0:
            xp[(per - 1 - j) * SEG:] = x[b, 0:j * SEG]
        m["xprev"] = xp
        xh = np.zeros((128, D), np.float32)
        if j > 0:
            xh[:] = x[b, j * SEG - 128:j * SEG]
        m["xhalo"] = xh
        m["hvalid"] = np.full((128, 1), 1.0 if j > 0 else 0.0, np.float32)
        in_maps.append(m)
    res = run_bass_kernel_spmd(nc, in_maps, core_ids=list(range(NCORE)))
    outp = np.empty((B, S, D), np.float32)
    for c in range(NCORE):
        b, j = c // per, c % per
        outp[b, j * SEG:(j + 1) * SEG] = res.results[c]["out"]
    return outp
```

```python
import os
import numpy as np
import ml_dtypes
from contextlib import ExitStack
import concourse.bass as bass
import concourse.mybir as mybir
from concourse.alu_op_type import AluOpType as ALU
from concourse.bass_utils import run_bass_kernel_spmd

F32 = mybir.dt.float32
BF16 = mybir.dt.bfloat16
AF = mybir.ActivationFunctionType

D = 2048
T = 512
NIN = 12816
OFF_GQ, OFF_GK, OFF_GV, OFF_GO, OFF_LR = 0, 1024, 2048, 4096, 6144
OFF_SQ, OFF_SK, OFF_SV, OFF_GA, OFF_GB = 6160, 8208, 8464, 8720, 10768
ALPHA = 2.0 ** 0.25
LN_EPS = 1e-5
RMS_EPS = 1e-6
ENGS = ("pe", "act", "dve", "pool", "sp")


class Buf:
    __slots__ = ("name", "w", "r", "dsem")

    def __init__(self, name):
        self.name = name
        self.w = None
        self.r = {}
        self.dsem = None


class Prog:
    def __init__(self, nc, stack):
        self.nc = nc
        self.stack = stack
        self.ops = {e: [] for e in ENGS}
        self.count = {}
        self.seen = {e: {} for e in ENGS}
        self.esem = {}
        self.nops = 0
        self.limit = None
        for e in ENGS:
            s = stack.enter_context(nc.semaphore("sem_" + e))
            self.esem[e] = s
            self.count[s] = 0

    def new_sem(self, name):
        s = self.stack.enter_context(self.nc.semaphore(name))
        self.count[s] = 0
        return s

    def _deps(self, eng, reads, writes):
        need = {}
        pe = eng == "pe"
        for b in reads:
            if b.w is not None:
                s, v, e = b.w
                if not (pe and e == "pe"):
                    if need.get(s, 0) < v:
                        need[s] = v
        for b in writes:
            if b.w is not None:
                s, v, e = b.w
                if not (pe and e == "pe"):
                    if need.get(s, 0) < v:
                        need[s] = v
            for s, (v, e) in b.r.items():
                if not (pe and e == "pe"):
                    if need.get(s, 0) < v:
                        need[s] = v
        waits = []
        seen = self.seen[eng]
        for s, v in need.items():
            if seen.get(s, 0) < v:
                waits.append((s, v))
                seen[s] = v
        return waits

    def _mark(self, tok, reads, writes):
        s, v, e = tok
        for b in reads:
            old = b.r.get(s)
            if old is None or old[0] < v:
                b.r[s] = (v, e)
        for b in writes:
            b.w = tok
            b.r = {}

    def op(self, eng, fn, reads=(), writes=(), sig=True):
        waits = self._deps(eng, reads, writes)
        s = self.esem[eng]
        if sig:
            self.count[s] += 1
            inc = (s, 1)
            tok = (s, self.count[s], eng)
        else:
            inc = None
            tok = (s, self.count[s] + 1, eng)
        self._mark(tok, reads, writes)
        self.nops += 1
        if self.limit is None or self.nops <= self.limit:
            self.ops[eng].append((fn, waits, inc))

    def dma(self, eng, fns, buf, reads=(), writes=()):
        if buf.dsem is None:
            buf.dsem = self.new_sem("d_" + buf.name)
        waits = self._deps(eng, reads, writes)
        s = buf.dsem
        self.nops += 1
        for i, fn in enumerate(fns):
            self.count[s] += 16
            if self.limit is None or self.nops <= self.limit:
                self.ops[eng].append((fn, waits if i == 0 else [], (s, 16)))
        self._mark((s, self.count[s], "dma"), reads, writes)

    def final_wait(self, eng, bufs):
        waits = self._deps(eng, bufs, bufs)
        if self.limit is None:
            self.ops[eng].append((None, waits, None))
        print("PROG nops", self.nops, {e: len(v) for e, v in self.ops.items()})

    def emit(self, block):
        handles = {"pe": "tensor", "act": "scalar", "dve": "vector", "pool": "gpsimd", "sp": "sync"}

        def make(e):
            def body(h):
                for fn, waits, inc in self.ops[e]:
                    for s, v in waits:
                        h.wait_ge(s, v)
                    if fn is None:
                        continue
                    ins = fn(h)
                    if inc is not None:
                        ins.then_inc(inc[0], inc[1])
            return body

        for e in ENGS:
            getattr(block, handles[e])(make(e))


class Region:
    def __init__(self, nc, stack, name, nbytes, gran):
        self.t = stack.enter_context(nc.sbuf_tensor(name, [128, nbytes // 2], BF16))
        self.gran = gran
        self.bufs = [Buf(f"{name}{i}") for i in range(nbytes // gran)]
        self.nbytes = nbytes

    def view(self, off, shape, dtype):
        esz = 4 if dtype == F32 else 2
        n = 1
        for s in shape:
            n *= s
        nb = n * esz
        assert off % 4 == 0 and off + nb <= self.nbytes, (off, nb, self.nbytes)
        ap = self.t[:, off // 2:(off + nb) // 2]
        if dtype == F32:
            ap = ap.bitcast(F32)
        if len(shape) == 2:
            ap = ap.rearrange("p (a b) -> p a b", a=shape[0])
        elif len(shape) == 3:
            ap = ap.rearrange("p (a b c) -> p a b c", a=shape[0], b=shape[1])
        bufs = self.bufs[off // self.gran:(off + nb - 1) // self.gran + 1]
        return ap, bufs


def build(NT_MAIN, NT_PRE):
    nc = bass.Bass("TRN2", target_bir_lowering=False)
    TOK = NT_MAIN * T
    PTOK = max(NT_PRE, 1) * T

    def din(name, shape, dt=F32):
        return nc.dram_tensor(name, shape, dt, kind="ExternalInput").ap()

    x = din("x", [TOK, D])
    xprev = din("xprev", [PTOK, D])
    xhalo = din("xhalo", [128, D])
    hvalid = din("hvalid", [128, 1])
    w_in = din("w_in", [D, NIN])
    w_alpha_up = din("w_alpha_up", [16, 1024])
    b_alpha = din("b_alpha", [1024])
    gla_norm_w = din("gla_norm_w", [512])
    attn_sinks = din("attn_sinks", [32])
    w_bg = din("w_branch_gla", [D, D])
    w_bs = din("w_branch_swa", [D, D])
    w_out = din("w_out", [D, D])
    ln1_g = din("ln1_g", [D])
    ln1_b = din("ln1_b", [D])
    w_up = din("w_ff_up", [D, 4 * D])
    w_dn = din("w_ff_down", [4 * D, D])
    ln2_g = din("ln2_g", [D])
    ln2_b = din("ln2_b", [D])
    swa_mask = din("swa_mask", [128, 4, 2048], BF16)
    out = nc.dram_tensor("out", [TOK, D], F32, kind="ExternalOutput").ap()

    with ExitStack() as st:
        P = Prog(nc, st)

        def sb(name, shape, dt):
            return st.enter_context(nc.sbuf_tensor(name, shape, dt))

        ident_f = sb("ident_f", [128, 128], F32)
        ident_b = sb("ident_b", [128, 128], BF16)
        cmask = sb("cmask", [128, 128], F32)
        ones_f = sb("ones_f", [128, 128], F32)
        S_f = sb("S_f", [128, 8, 512], F32)
        S_b = sb("S_b", [128, 8, 512], BF16)
        w_lr = sb("w_lr", [128, 16, 16], BF16)
        w_au = sb("w_au", [16, 1024], BF16)
        nba = sb("nba", [128, 8], F32)
        gw_bc = sb("gw_bc", [128, 512], F32)
        sinkexp = sb("sinkexp", [128, 32], F32)
        hval = sb("hval", [128, 1], F32)
        gb = sb("gb", [128, 2, D], F32)
        kT_carry = sb("kT_carry", [128, 4, 128], BF16)
        v_carry = sb("v_carry", [128, 4, 65], BF16)
        lrT = sb("lrT", [16, T], BF16)
        rtmp = sb("rtmp", [128, 2, 512], F32)
        small = sb("small", [128, 64], F32)
        stats = sb("stats", [128, 4, 6], F32)
        xT = sb("xT", [128, 16, T], BF16)
        RX = Region(nc, st, "RX", 32 * 1024, 1024)
        RB = Region(nc, st, "RB", 96 * 1024, 1024)

        B_ident_f, B_ident_b, B_cmask, B_ones = Buf("ident_f"), Buf("ident_b"), Buf("cmask"), Buf("ones")
        B_S_f = [Buf(f"S_f{i}") for i in range(8)]
        B_S_b = [Buf(f"S_b{i}") for i in range(8)]
        B_wlr, B_wau, B_nba, B_gw, B_sink, B_hval = (Buf("wlr"), Buf("wau"), Buf("nba"), Buf("gw"),
                                                      Buf("sink"), Buf("hval"))
        B_gb, B_kc, B_vc, B_lrT = Buf("gb"), Buf("kc"), Buf("vc"), Buf("lrT")
        B_rtmp = [Buf("rtmp0"), Buf("rtmp1")]
        B_small = {}
        B_stats = Buf("stats")
        B_xT = [Buf(f"xT{i}") for i in range(16)]
        B_xload, B_xstore, B_gbload, B_mskload = Buf("xload"), Buf("xstore"), Buf("gbload"), Buf("mskload")
        B_xstore4 = [Buf(f"xstore{i}") for i in range(4)]

        def smallv(name, idx, n=1):
            if name not in B_small:
                B_small[name] = Buf("sm_" + name)
            return small[:, idx:idx + n], B_small[name]

        psM = [st.enter_context(nc.psum_tensor(f"psM{i}", [128, 512], F32)) for i in range(4)]
        psA = [st.enter_context(nc.psum_tensor(f"psA{i}", [128, 512], F32)) for i in range(3)]
        psT = st.enter_context(nc.psum_tensor("psT", [128, 1024], BF16))
        B_psM = [Buf(f"psM{i}") for i in range(4)]
        B_psA = [Buf(f"psA{i}") for i in range(3)]
        B_psT = Buf("psT")
        psT_f = psT[:, :].bitcast(F32)
        psA_bf = [psA[i][:, :].bitcast(BF16) for i in range(3)]

        def mm(o, lhsT, rhs, start, stop, reads, writes, sig):
            P.op("pe", lambda h: h.matmul(o, lhsT=lhsT, rhs=rhs, start=start, stop=stop),
                 reads=reads, writes=writes, sig=sig)

        def tr(o, in_, idt, reads, writes, sig):
            P.op("pe", lambda h: h.transpose(out=o, in_=in_, identity=idt), reads=reads, writes=writes, sig=sig)

        def act(o, in_, func, reads, writes, scale=None, bias=None, accum=None):
            kw = {}
            if scale is not None:
                kw["scale"] = scale
            if bias is not None:
                kw["bias"] = bias
            if accum is not None:
                kw["accum_out"] = accum
            P.op("act", lambda h: h.activation(out=o, in_=in_, func=func, **kw), reads=reads, writes=writes)

        def tt(o, a, b, op, reads, writes, eng="dve"):
            P.op(eng, lambda h: h.tensor_tensor(out=o, in0=a, in1=b, op=op), reads=reads, writes=writes)

        def stt(o, a, scalar, b, op0, op1, reads, writes):
            P.op("dve", lambda h: h.scalar_tensor_tensor(out=o, in0=a, scalar=scalar, in1=b, op0=op0, op1=op1),
                 reads=reads, writes=writes)

        def ts(o, a, s1, s2, op0, op1, reads, writes):
            if op1 is None:
                P.op("dve", lambda h: h.tensor_scalar(out=o, in0=a, scalar1=s1, scalar2=None, op0=op0),
                     reads=reads, writes=writes)
            else:
                P.op("dve", lambda h: h.tensor_scalar(out=o, in0=a, scalar1=s1, scalar2=s2, op0=op0, op1=op1),
                     reads=reads, writes=writes)

        def cp(eng, o, a, reads, writes):
            if eng == "act":
                P.op("act", lambda h: h.activation(out=o, in_=a, func=AF.Copy), reads=reads, writes=writes)
            else:
                P.op("dve", lambda h: h.tensor_copy(out=o, in_=a), reads=reads, writes=writes)

        evac_rr = [0]

        def cp_rr(o, a, reads, writes):
            evac_rr[0] ^= 1
            cp("act" if evac_rr[0] else "dve", o, a, reads, writes)

        NSLOT = 4
        slot_ap = []
        slot_buf = []
        for i in range(NSLOT):
            ap_, bufs_ = RB.view((8 + i) * 8192, [8, 512], BF16)
            slot_ap.append(ap_)
            slot_buf.append(bufs_)
        wstate = [0]

        def wget(w, k0, pieces, nk=8):
            i = wstate[0] % NSLOT
            wstate[0] += 1
            sl, bf = slot_ap[i], slot_buf[i]
            fns = []
            co = 0
            for (c0, n) in pieces:
                for kh in range(0, nk, 4):
                    kn = min(4, nk - kh)
                    src = w[(k0 + kh) * 128:(k0 + kh + kn) * 128, c0:c0 + n].rearrange("(k p) c -> p k c", p=128)
                    dst = sl[:, kh:kh + kn, co:co + n]
                    fns.append(lambda h, dst=dst, src=src: h.dma_start(out=dst, in_=src))
                co += n
            P.dma("pool", fns, bf[0], writes=bf)
            return sl, bf

        def lin_fm(w, pieces, nchunk, rhs_fn, evac, nkb=2, resident=None, defer=False):
            for kb in range(nkb):
                if resident is None:
                    sl, bf = wget(w, kb * 8, pieces)
                for oc in range(nchunk):
                    for k in range(8):
                        kk = kb * 8 + k
                        r_ap, r_bufs = rhs_fn(kk)
                        if resident is None:
                            l_ap, l_bufs = sl[:, k, oc * 128:(oc + 1) * 128], bf[k:k + 1]
                        else:
                            l_ap, l_bufs = resident(kk, oc)
                        n = r_ap.shape[-1]
                        mm(psM[oc][:, 0:n], l_ap, r_ap, kk == 0, kk == nkb * 8 - 1,
                           reads=l_bufs + r_bufs, writes=[B_psM[oc]],
                           sig=(kk == nkb * 8 - 1) or (k == 7 and oc == nchunk - 1))
            if defer:
                return [lambda oc=oc: evac(oc, psM[oc], B_psM[oc]) for oc in range(nchunk)]
            for oc in range(nchunk):
                evac(oc, psM[oc], B_psM[oc])

        def lin_tm(w, c0, ncols, lhs_fn, evac, nkb=2, nsub=4, resident=None):
            for kb in range(nkb):
                if resident is None:
                    sl, bf = wget(w, kb * 8, [(c0, ncols)])
                for sub in range(nsub):
                    for k in range(8):
                        kk = kb * 8 + k
                        l_ap, l_bufs = lhs_fn(kk, sub)
                        if resident is None:
                            r_ap, r_bufs = sl[:, k, 0:ncols], bf[k:k + 1]
                        else:
                            r_ap, r_bufs = resident(kk)
                        mm(psM[sub][:, 0:ncols], l_ap, r_ap, kk == 0, kk == nkb * 8 - 1,
                           reads=l_bufs + r_bufs, writes=[B_psM[sub]],
                           sig=(kk == nkb * 8 - 1) or (k == 7 and sub == nsub - 1))
            for sub in range(nsub):
                evac(sub, psM[sub], B_psM[sub])

        P.op("pool", lambda h: h.memset(ident_f[:], 1.0), writes=[B_ident_f])
        P.op("pool", lambda h: h.affine_select(out=ident_f[:], in_=ident_f[:], pattern=[[-1, 128]],
                                               compare_op=ALU.is_equal, fill=0.0, base=0, channel_multiplier=1),
             reads=[B_ident_f], writes=[B_ident_f])
        P.op("pool", lambda h: h.memset(cmask[:], 1.0), writes=[B_cmask])
        P.op("pool", lambda h: h.affine_select(out=cmask[:], in_=cmask[:], pattern=[[1, 128]],
                                               compare_op=ALU.is_ge, fill=0.0, base=0, channel_multiplier=-1),
             reads=[B_cmask], writes=[B_cmask])
        P.op("pool", lambda h: h.memset(ones_f[:], 1.0), writes=[B_ones])
        cp("dve", ident_b[:], ident_f[:], [B_ident_f], [B_ident_b])
        for i in range(8):
            P.op("dve", lambda h, i=i: h.memset(S_f[:, i, :], 0.0), writes=[B_S_f[i]])
            P.op("dve", lambda h, i=i: h.memset(S_b[:, i, :], 0.0), writes=[B_S_b[i]])
        P.dma("pool", [lambda h: h.dma_start(out=w_lr[:], in_=w_in[:, OFF_LR:OFF_LR + 16].rearrange("(k p) c -> p k c", p=128))],
              B_wlr, writes=[B_wlr])
        P.dma("pool", [lambda h: h.dma_start(out=w_au[:], in_=w_alpha_up[:, :])], B_wau, writes=[B_wau])
        P.dma("sp", [lambda h: h.dma_start(out=nba[:], in_=b_alpha.rearrange("(c p) -> p c", p=128),
                                            allow_slow_non_contiguous=True)], B_nba, writes=[B_nba])
        ts(nba[:], nba[:], -1.0, None, ALU.mult, None, [B_nba], [B_nba])
        P.dma("sp", [lambda h: h.dma_start(out=gw_bc[:], in_=gla_norm_w.partition_broadcast(128))], B_gw, writes=[B_gw])
        P.dma("sp", [lambda h: h.dma_start(out=sinkexp[:], in_=attn_sinks.partition_broadcast(128))], B_sink, writes=[B_sink])
        act(sinkexp[:], sinkexp[:], AF.Exp, [B_sink], [B_sink])
        P.dma("sp", [lambda h: h.dma_start(out=hval[:], in_=hvalid[:, :])], B_hval, writes=[B_hval])
        g1T = sb("g1T", [128, 16], F32)
        b1T = sb("b1T", [128, 16], F32)
        B_g1T = Buf("g1T")
        P.dma("sp", [lambda h: h.dma_start(out=g1T[:], in_=ln1_g.rearrange("(c p) -> p c", p=128), allow_slow_non_contiguous=True),
                     lambda h: h.dma_start(out=b1T[:], in_=ln1_b.rearrange("(c p) -> p c", p=128), allow_slow_non_contiguous=True)],
              B_g1T, writes=[B_g1T])

        xs_ap, xs_bufs = RX.view(0, [4, D], F32)

        def load_x_tile(src, ntok=T):
            ns = ntok // 128
            P.dma("sp", [lambda h, s=s: h.dma_start(out=xs_ap[:, s, :], in_=src[s * 128:(s + 1) * 128, :]) for s in range(ns)],
                  B_xload, writes=xs_bufs[:8 * ns])

        def transpose_to_xT(ns=4, src=None, src_bufs=None, affine=False):
            if src is None:
                src, src_bufs = xs_ap, xs_bufs
            for c in range(16):
                pa = psA[c % 2]
                bpa = B_psA[c % 2]
                for s in range(ns):
                    tr(pa[:, s * 128:(s + 1) * 128], src[:, s, c * 128:(c + 1) * 128], ident_f[:],
                       reads=src_bufs[s * 8 + c // 2: s * 8 + c // 2 + 1] + [B_ident_f], writes=[bpa], sig=(s == ns - 1))
                if not affine:
                    cp_rr(xT[:, c, 0:ns * 128], pa[:, 0:ns * 128], [bpa], [B_xT[c]])
                elif c % 2 == 0:
                    act(xT[:, c, :], pa[:, :], AF.Identity, [bpa, B_g1T], [B_xT[c]], scale=g1T[:, c:c + 1], bias=b1T[:, c:c + 1])
                else:
                    ts(xT[:, c, :], pa[:, :], g1T[:, c:c + 1], b1T[:, c:c + 1], ALU.mult, ALU.add, [bpa, B_g1T], [B_xT[c]])

        def xT_rhs(k):
            return xT[:, k, :], [B_xT[k]]

        def xT_lhs(k, sub):
            return xT[:, k, sub * 128:(sub + 1) * 128], [B_xT[k]]

        class GV:
            pass

        def make_gla_views(reg, base):
            g = GV()
            g.E1, g.bE1 = reg.view(base + 0, [2, 512], F32)
            g.E2, g.bE2 = reg.view(base + 4096, [2, 512], F32)
            g.la, g.bla = reg.view(base + 8192, [512], F32)
            g.bp, g.bbp = reg.view(base + 10240, [512], F32)
            g.dec, g.bdec = reg.view(base + 12288, [2, 4], F32)
            g.qinT, g.bqin = reg.view(base + 14336, [2, 512], BF16)
            g.kinT, g.bkin = reg.view(base + 16384, [2, 512], BF16)
            g.kdecT, g.bkdecT = reg.view(base + 18432, [2, 512], BF16)
            g.kdec_tm, g.bkdtm = reg.view(base + 20480, [2, 4, 128], BF16)
            g.v_tm, g.bvtm = reg.view(base + 22528, [4, 512], BF16)
            g.gsw, g.bgsw = reg.view(base + 26624, [4, 512], BF16)
            g.o_g, g.bog = reg.view(base + 30720, [512], BF16)
            g.sT, g.bsT = reg.view(base + 31744, [128], BF16)
            return g

        GSETS = [make_gla_views(RX, 0), make_gla_views(RB, 16384)]
        ssq, b_ssq = smallv("ssq", 0)
        rst, b_rst = smallv("rst", 1)

        def gla_lr():
            for k in range(16):
                mm(psA[2][0:16, :], w_lr[:, k, :], xT[:, k, :], k == 0, k == 15,
                   reads=[B_wlr, B_xT[k]], writes=[B_psA[2]], sig=(k == 15))
            cp("act", lrT[:], psA[2][0:16, :], [B_psA[2]], [B_lrT])

        def gla_decay(g, h_, c2):
            gc = h_ * 2 + c2
            zb, bzb = (psA[2][:, :], B_psA[2]) if c2 == 0 else (psT_f, B_psT)
            mm(zb, w_au[0:16, gc * 128:(gc + 1) * 128], lrT[0:16, :], True, True,
               reads=[B_wau, B_lrT], writes=[bzb], sig=True)
            act(g.la, zb, AF.Exp, [bzb, B_nba], g.bla, scale=-1.0, bias=nba[:, gc:gc + 1])
            act(g.la, g.la, AF.Ln, g.bla, g.bla, bias=1.0)
            for ch in range(4):
                P.op("dve", lambda h, ch=ch: h.tensor_tensor_scan(out=g.bp[:, ch * 128:(ch + 1) * 128], data0=ones_f[:, :],
                                                                  data1=g.la[:, ch * 128:(ch + 1) * 128], initial=0.0,
                                                                  op0=ALU.mult, op1=ALU.add),
                     reads=g.bla + [B_ones], writes=g.bbp)
            act(g.E1[:, c2, :], g.bp, AF.Exp, g.bbp, g.bE1[c2 * 2:c2 * 2 + 2], scale=-1.0 / 16)
            act(g.E2[:, c2, :], g.bp, AF.Exp, g.bbp, g.bE2[c2 * 2:c2 * 2 + 2], scale=1.0 / 16)
            act(g.dec[:, c2, :], g.bp[:, 127::128], AF.Exp, g.bbp, g.bdec, scale=-1.0 / 16)

        def lin_fm_gen(w, pieces, nchunk, rhs_fn, evac, nkb=2):
            for kb in range(nkb):
                sl, bf = wget(w, kb * 8, pieces)
                for oc in range(nchunk):
                    for k in range(8):
                        kk = kb * 8 + k
                        r_ap, r_bufs = rhs_fn(kk)
                        mm(psM[oc][:, :], sl[:, k, oc * 128:(oc + 1) * 128], r_ap, kk == 0, kk == nkb * 8 - 1,
                           reads=bf[k:k + 1] + r_bufs, writes=[B_psM[oc]],
                           sig=(kk == nkb * 8 - 1) or (k == 7 and oc == nchunk - 1))
                if kb == nkb - 1:
                    for oc in range(nchunk):
                        evac(oc, psM[oc], B_psM[oc])
                yield

        def lin_tm_gen(w, c0, ncols, lhs_fn, evac, nkb=2, nsub=4):
            for kb in range(nkb):
                sl, bf = wget(w, kb * 8, [(c0, ncols)])
                for sub in range(nsub):
                    for k in range(8):
                        kk = kb * 8 + k
                        l_ap, l_bufs = lhs_fn(kk, sub)
                        mm(psM[sub][:, 0:ncols], l_ap, sl[:, k, 0:ncols], kk == 0, kk == nkb * 8 - 1,
                           reads=l_bufs + bf[k:k + 1], writes=[B_psM[sub]],
                           sig=(kk == nkb * 8 - 1) or (k == 7 and sub == nsub - 1))
                if kb == nkb - 1:
                    for sub in range(nsub):
                        evac(sub, psM[sub], B_psM[sub])
                yield

        def gla_stage1(h_, g):
            pieces = [(OFF_GQ + h_ * 256, 256), (OFF_GK + h_ * 256, 256)]
            if h_ != 0:
                for c2 in range(2):
                    gla_decay(g, h_, c2)

            def evac_qk(oc, ps, bps):
                if oc < 2:
                    c2 = oc
                    stt(g.qinT[:, c2, :], ps[:, :], 1.0 / 16, g.E1[:, c2, :], ALU.mult, ALU.mult,
                        [bps] + g.bE1[c2 * 2:c2 * 2 + 2], g.bqin[c2:c2 + 1])
                else:
                    c2 = oc - 2
                    tt(g.kinT[:, c2, :], ps[:, :], g.E2[:, c2, :], ALU.mult, [bps] + g.bE2[c2 * 2:c2 * 2 + 2], g.bkin[c2:c2 + 1])
                    for ch in range(4):
                        stt(g.kdecT[:, c2, ch * 128:(ch + 1) * 128], g.E2[:, c2, ch * 128:(ch + 1) * 128], g.dec[:, c2, ch:ch + 1],
                            ps[:, ch * 128:(ch + 1) * 128], ALU.mult, ALU.mult,
                            [bps] + g.bE2[c2 * 2:c2 * 2 + 2] + g.bdec, g.bkdecT[c2:c2 + 1])

            yield from lin_fm_gen(w_in, pieces, 4, xT_rhs, evac_qk)

            def evac_v(sub, ps, bps):
                cp_rr(g.v_tm[:, sub, :], ps[:, :], [bps], g.bvtm[sub:sub + 1])

            yield from lin_tm_gen(w_in, OFF_GV + h_ * 512, 512, xT_lhs, evac_v)

            def evac_g(sub, ps, bps):
                j = sub % 2
                act(rtmp[:, j, :], ps[:, :], AF.Silu, [bps], [B_rtmp[j]])
                tt(g.gsw[:, sub, :], rtmp[:, j, :], gw_bc[:, :], ALU.mult, [B_rtmp[j], B_gw], g.bgsw[sub:sub + 1])

            yield from lin_tm_gen(w_in, OFF_GO + h_ * 512, 512, xT_lhs, evac_g)
            for c2 in range(2):
                for sub in range(4):
                    j = c2 * 4 + sub
                    tr(psT[:, j * 128:(j + 1) * 128], g.kdecT[:, c2, sub * 128:(sub + 1) * 128], ident_b[:],
                       reads=g.bkdecT[c2:c2 + 1] + [B_ident_b], writes=[B_psT], sig=(j == 7))
            cp("act", g.kdec_tm.rearrange("p a b c -> p (a b c)"), psT[:, :], [B_psT], g.bkdtm)
            yield

        def gla_stage2(h_, g):
            def ph_a(ch):
                cs = slice(ch * 128, (ch + 1) * 128)
                for c2 in range(2):
                    mm(psA[0][:, 0:128], g.kinT[:, c2, cs], g.qinT[:, c2, cs], c2 == 0, c2 == 1,
                       reads=g.bkin[c2:c2 + 1] + g.bqin[c2:c2 + 1], writes=[B_psA[0]], sig=(c2 == 1))
                tt(g.sT, psA[0][:, 0:128], cmask[:, :], ALU.mult, [B_psA[0], B_cmask], g.bsT)

            def ph_b(ch):
                cs = slice(ch * 128, (ch + 1) * 128)
                for c2 in range(2):
                    mm(psA[1][:, :], g.qinT[:, c2, cs], S_b[:, h_ * 2 + c2, :], c2 == 0, False,
                       reads=g.bqin[c2:c2 + 1] + [B_S_b[h_ * 2 + c2]], writes=[B_psA[1]], sig=False)
                mm(psA[1][:, :], g.sT, g.v_tm[:, ch, :], False, True,
                   reads=g.bsT + g.bvtm[ch:ch + 1], writes=[B_psA[1]], sig=True)
                for c2 in range(2):
                    gi = h_ * 2 + c2
                    ub, bub = psA[2], B_psA[2]
                    mm(ub[:, :], g.kdec_tm[:, c2, ch, :], g.v_tm[:, ch, :], True, True,
                       reads=g.bkdtm + g.bvtm[ch:ch + 1], writes=[bub], sig=True)
                    stt(S_f[:, gi, :], S_f[:, gi, :], g.dec[:, c2, ch:ch + 1], ub[:, :], ALU.mult, ALU.add,
                        [B_S_f[gi], bub] + g.bdec, [B_S_f[gi]])
                    cp("act", S_b[:, gi, :], S_f[:, gi, :], [B_S_f[gi]], [B_S_b[gi]])
                act(g.o_g, psA[1][:, :], AF.Square, [B_psA[1]], g.bog + [b_ssq], accum=ssq)
                act(rst, ssq, AF.Ln, [b_ssq], [b_rst], scale=1.0 / 512, bias=RMS_EPS)
                act(rst, rst, AF.Exp, [b_rst], [b_rst], scale=-0.5)
                stt(g.o_g, psA[1][:, :], rst, g.gsw[:, ch, :], ALU.mult, ALU.mult,
                    [B_psA[1], b_rst] + g.bgsw[ch:ch + 1], g.bog)

            def ph_c(ch):
                cs = slice(ch * 128, (ch + 1) * 128)
                for j in range(4):
                    tr(psT[:, j * 128:(j + 1) * 128], g.o_g[:, j * 128:(j + 1) * 128], ident_b[:],
                       reads=g.bog + [B_ident_b], writes=[B_psT], sig=(j == 3))
                cp("dve", o_aT[:, h_ * 4:(h_ + 1) * 4, cs], psT[:, 0:512].rearrange("p (a b) -> p a b", a=4),
                   [B_psT], bo_aT[h_ * 4:(h_ + 1) * 4])

            for step in range(6):
                if 0 <= step - 2 < 4:
                    ph_c(step - 2)
                if 0 <= step - 1 < 4:
                    ph_b(step - 1)
                if step < 4:
                    ph_a(step)
                yield

        def exhaust(gen):
            for _ in gen:
                pass

        def interleave(ga, gb_, nb=1):
            da = db = False
            while not (da and db):
                if not da:
                    try:
                        next(ga)
                    except StopIteration:
                        da = True
                for _ in range(nb):
                    if not db:
                        try:
                            next(gb_)
                        except StopIteration:
                            db = True

        def gla_pre():
            gla_lr()
            for c2 in range(2):
                gla_decay(GSETS[1], 0, c2)

        def gla_tile(pending=None):
            if pending is None:
                exhaust(gla_stage1(0, GSETS[1]))
            else:
                interleave(gla_stage1(0, GSETS[1]), pending, nb=2)
            for h_ in range(1, 4):
                interleave(gla_stage1(h_, GSETS[(h_ + 1) % 2]), gla_stage2(h_ - 1, GSETS[h_ % 2]))
            return gla_stage2(3, GSETS[0])

        B_wk1l, B_wv1l = Buf("wk1l"), Buf("wv1l")
        o_aT, bo_aT = RB.view(0, [16, 512], BF16)
        o_bT, bo_bT = RB.view(16384, [16, 512], BF16)
        mgT, bmgT = RB.view(32768, [16, 512], BF16)
        t1, bt1 = RB.view(49152, [4, 512], F32)
        t2, bt2 = RB.view(57344, [512], F32)
        hT, bhT = RB.view(0, [64, 512], BF16)

        if NT_PRE > 0:
            wk1, bwk1 = RB.view(8 * 8192, [16, 1024], BF16)
            Mv, bM = RB.view(0, [8, 2048], F32)

            class PV:
                pass

            pa_ = PV()
            pa_.E2p, bE2pA = RX.view(0, [2, 512], F32)
            pa_.bE2 = lambda c2, b=bE2pA: b[c2 * 2:c2 * 2 + 2]
            pa_.bpp, bbppA = RX.view(6144, [2, 512], F32)
            pa_.bbp = lambda c2, b=bbppA: b[c2 * 2:c2 * 2 + 2]
            pa_.bbp_all = bbppA
            pa_.kdecTp, bkdA = RX.view(10240, [2, 512], BF16)
            pa_.bkdT = lambda c2, b=bkdA: b[c2:c2 + 1]
            pa_.kdtmp, pa_.bkdtm = RX.view(12288, [2, 4, 128], BF16)
            pa_.decp, pa_.bdecp = RX.view(14336, [2, 4], F32)
            pb_ = PV()
            pb_.E2p = S_f[:, 0:2, :]
            pb_.bE2 = lambda c2: [B_S_f[c2]]
            pb_.bpp = S_f[:, 2:4, :]
            pb_.bbp = lambda c2: [B_S_f[2 + c2]]
            pb_.bbp_all = [B_S_f[2], B_S_f[3]]
            pb_.kdecTp = S_f[:, 4, :].bitcast(BF16).rearrange("p (a b) -> p a b", a=2)[:, :, 0:512]
            pb_.bkdT = lambda c2: [B_S_f[4]]
            pb_.kdtmp = S_f[:, 5, :].bitcast(BF16).rearrange("p (a b c) -> p a b c", a=2, b=4)
            pb_.bkdtm = [B_S_f[5]]
            pb_.decp = S_f[:, 6, 0:8].rearrange("p (a b) -> p a b", a=2)
            pb_.bdecp = [B_S_f[6]]
            PSETS = [pa_, pb_]
            lap, blap = RX.view(4096, [512], F32)
            xbfB, bxbfB = RX.view(16384, [4, 2048], BF16)
            xbfA = gb[:, :, :].rearrange("p a b -> p (a b)").bitcast(BF16).rearrange("p (s d) -> p s d", s=4)
            xbfs = [(xbfA, [B_gb], Buf("xbfA_l")), (xbfB, bxbfB, Buf("xbfB_l"))]
            Rt, b_Rt = smallv("Rt", 24, 8)
            blv, b_blv = smallv("blv", 32, 8)
            exv, b_exv = smallv("exv", 40, 8)
            blv3 = blv.rearrange("p (a b) -> p a b", a=2)
            exv3 = exv.rearrange("p (a b) -> p a b", a=2)
            fns = []
            for kq in range(4):
                fns.append(lambda h, kq=kq: h.dma_start(
                    out=wk1[:, kq * 4:(kq + 1) * 4, :],
                    in_=w_in[kq * 512:(kq + 1) * 512, OFF_GK:OFF_GK + 1024].rearrange("(k p) c -> p k c", p=128)))
            P.dma("pool", fns, B_wk1l, writes=bwk1)
            for gc in range(8):
                P.op("dve", lambda h, gc=gc: h.memset(Mv[:, gc, :], 0.0), writes=bM[gc * 8:(gc + 1) * 8])
            P.op("dve", lambda h: h.memset(Rt, 0.0), writes=[b_Rt])
            grp = [0]

            def pre_prologue(tp):
                t = NT_PRE - 1 - tp
                xbf, bxbf, bxl = xbfs[tp % 2]
                P.dma("pool", [lambda h, s_=s_, xbf=xbf, t=t: h.dma_start(out=xbf[:, s_, :], in_=xprev[t * T + s_ * 128:t * T + (s_ + 1) * 128, :])
                               for s_ in range(4)], bxl, writes=bxbf)
                banks = [(psT[:, :], B_psT), (psA_bf[2], B_psA[2])]
                for cp2 in range(8):
                    bk, bbk = banks[cp2 % 2]
                    for cc in range(2):
                        c = cp2 * 2 + cc
                        for s_ in range(4):
                            tr(bk[:, cc * 512 + s_ * 128: cc * 512 + (s_ + 1) * 128], xbf[:, s_, c * 128:(c + 1) * 128], ident_b[:],
                               reads=bxbf + [B_ident_b], writes=[bbk], sig=(cc == 1 and s_ == 3))
                    cp_rr(xT[:, cp2 * 2:cp2 * 2 + 2, :], bk.rearrange("p (a b) -> p a b", a=2), [bbk],
                          [B_xT[cp2 * 2], B_xT[cp2 * 2 + 1]])
                gla_lr()

            def pre_stage1(tp, h_, v):
                for c2 in range(2):
                    gc = h_ * 2 + c2
                    zb, bzb = (psA[2][:, :], B_psA[2]) if c2 == 0 else (psT_f, B_psT)
                    mm(zb, w_au[0:16, gc * 128:(gc + 1) * 128], lrT[0:16, :], True, True,
                       reads=[B_wau, B_lrT], writes=[bzb], sig=True)
                    act(lap, zb, AF.Exp, [bzb, B_nba], blap, scale=-1.0, bias=nba[:, gc:gc + 1])
                    act(lap, lap, AF.Ln, blap, blap, bias=1.0)
                    for ch in range(4):
                        P.op("dve", lambda h, ch=ch, c2=c2: h.tensor_tensor_scan(
                            out=v.bpp[:, c2, ch * 128:(ch + 1) * 128], data0=ones_f[:, :],
                            data1=lap[:, ch * 128:(ch + 1) * 128], initial=0.0, op0=ALU.mult, op1=ALU.add),
                            reads=blap + [B_ones], writes=v.bbp(c2))
                ts(blv3, v.bpp[:, :, 127::128], -1.0 / 16, None, ALU.mult, None, v.bbp_all, [b_blv])
                Rh = Rt[:, h_ * 2:h_ * 2 + 2].unsqueeze(2)
                cp("dve", exv3[:, :, 3:4], Rh, [b_Rt], [b_exv])
                for ch in (2, 1, 0):
                    tt(exv3[:, :, ch:ch + 1], exv3[:, :, ch + 1:ch + 2], blv3[:, :, ch + 1:ch + 2], ALU.add, [b_exv, b_blv], [b_exv])
                tt(Rh, exv3[:, :, 0:1], blv3[:, :, 0:1], ALU.add, [b_exv, b_blv], [b_Rt])
                tt(exv, exv, blv, ALU.add, [b_exv, b_blv], [b_exv])
                for c2 in range(2):
                    for ch in range(4):
                        act(v.E2p[:, c2, ch * 128:(ch + 1) * 128], v.bpp[:, c2, ch * 128:(ch + 1) * 128], AF.Exp,
                            v.bbp(c2) + [b_exv], v.bE2(c2), scale=1.0 / 16, bias=exv3[:, c2, ch:ch + 1])

                def evac_kp(oc, ps, bps):
                    tt(v.kdecTp[:, oc, :], ps[:, :], v.E2p[:, oc, :], ALU.mult, [bps] + v.bE2(oc), v.bkdT(oc))

                return lin_fm(w_in, None, 2, xT_rhs, evac_kp, defer=True,
                              resident=lambda kk, oc, h_=h_: (wk1[:, kk, h_ * 256 + oc * 128: h_ * 256 + (oc + 1) * 128], bwk1[kk * 2:kk * 2 + 2]))

            def pre_stage2a(tp, h_, v):
                for c2 in range(2):
                    for sub in range(4):
                        j = c2 * 4 + sub
                        tr(psT[:, j * 128:(j + 1) * 128], v.kdecTp[:, c2, sub * 128:(sub + 1) * 128], ident_b[:],
                           reads=v.bkdT(c2) + [B_ident_b], writes=[B_psT], sig=(j == 7))
                cp("act", v.kdtmp.rearrange("p a b c -> p (a b c)"), psT[:, :], [B_psT], v.bkdtm)

            def pre_stage2b(tp, h_, v, pend=()):
                xbf, bxbf, bxl = xbfs[tp % 2]
                ngrp = 0
                for c2 in range(2):
                    gc = h_ * 2 + c2
                    for dpair in range(2):
                        if grp[0] % 2 == 0:
                            bk, bkb = [psM[2], psM[3]], [B_psM[2], B_psM[3]]
                        else:
                            bk, bkb = [psA[0], psA[1]], [B_psA[0], B_psA[1]]
                        grp[0] += 1
                        for db in range(2):
                            col0 = (dpair * 2 + db) * 512
                            for sub in range(4):
                                mm(bk[db][:, :], v.kdtmp[:, c2, sub, :], xbf[:, sub, col0:col0 + 512], sub == 0, sub == 3,
                                   reads=v.bkdtm + bxbf, writes=[bkb[db]], sig=(sub == 3))
                        for db in range(2):
                            col0 = (dpair * 2 + db) * 512
                            mb = bM[gc * 8 + (dpair * 2 + db) * 2: gc * 8 + (dpair * 2 + db) * 2 + 2]
                            tt(Mv[:, gc, col0:col0 + 512], Mv[:, gc, col0:col0 + 512], bk[db][:, :], ALU.add,
                               [bkb[db]] + mb, mb)
                        ngrp += 1
                        if ngrp == 2:
                            for th in pend:
                                th()

            items = [(tp, h_) for tp in range(NT_PRE) for h_ in range(4)]
            for i in range(len(items) + 1):
                pend = ()
                if i < len(items):
                    tp, h_ = items[i]
                    if h_ == 0:
                        pre_prologue(tp)
                    pend = pre_stage1(tp, h_, PSETS[i % 2])
                if i >= 1:
                    tp, h_ = items[i - 1]
                    pre_stage2b(tp, h_, PSETS[(i - 1) % 2], pend)
                else:
                    for th in pend:
                        th()
                if i < len(items):
                    tp, h_ = items[i]
                    pre_stage2a(tp, h_, PSETS[i % 2])
            MT, bMT = RX.view(0, [16, 1024], BF16)
            q4 = 0
            for gc in range(8):
                for d4 in range(4):
                    pa, bpa = psA[q4 % 2], B_psA[q4 % 2]
                    q4 += 1
                    for j in range(4):
                        dc = d4 * 4 + j
                        tr(pa[:, j * 128:(j + 1) * 128], Mv[:, gc, dc * 128:(dc + 1) * 128], ident_f[:],
                           reads=bM[gc * 8 + dc // 2: gc * 8 + dc // 2 + 1] + [B_ident_f], writes=[bpa], sig=(j == 3))
                    cp_rr(MT[:, d4 * 4:(d4 + 1) * 4, gc * 128:(gc + 1) * 128], pa[:, :].rearrange("p (a b) -> p a b", a=4),
                          [bpa], bMT[d4 * 8:(d4 + 1) * 8])
            for h_ in range(4):
                def evac_s0(sub, ps, bps, h_=h_):
                    gi = h_ * 2 + sub
                    cp("dve", S_f[:, gi, :], ps[:, :], [bps], [B_S_f[gi]])
                    cp("act", S_b[:, gi, :], S_f[:, gi, :], [B_S_f[gi]], [B_S_b[gi]])

                lin_tm(w_in, OFF_GV + h_ * 512, 512,
                       lambda k, sub, h_=h_: (MT[:, k, (h_ * 2 + sub) * 128:(h_ * 2 + sub + 1) * 128], bMT[k * 2:k * 2 + 2]),
                       evac_s0, nsub=2)

        qT_s2 = [RB.view(32768, [4, 512], BF16), RB.view(36864, [4, 512], BF16)]
        kT2, bkT2 = RB.view(40960, [4, 640], BF16)
        v_aug, bvaug = RB.view(46080, [5, 4, 65], BF16)
        msk2 = [RB.view(49152, [2048], BF16), RB.view(53248, [2048], BF16)]
        pT2 = [RB.view(57344, [1024], BF16), RB.view(59392, [1024], BF16)]
        obt2 = [RB.view(61440, [256], BF16), RB.view(62464, [256], BF16)]
        rden, b_rden = smallv("rden", 8, 8)
        B_mskl = [Buf("mskl0"), Buf("mskl1")]
        k_pieces = []
        for kv in range(4):
            k_pieces += [(OFF_SK + kv * 64, 64), (OFF_SK + kv * 64, 64)]

        def swa_kv_proj(ntok, k_dst, k_dst_bufs, v_dst_fn):
            def rhs(k):
                return xT[:, k, 0:ntok], [B_xT[k]]

            def evac_k(oc, ps, bps):
                cp_rr(k_dst(oc), ps[:, 0:ntok], [bps], k_dst_bufs)

            lin_fm(w_in, k_pieces, 4, rhs, evac_k)

            def evac_v(sub, ps, bps):
                o_, ob_ = v_dst_fn(sub)
                cp_rr(o_, ps[:, 0:256].rearrange("p (a b) -> p a b", a=4), [bps], ob_)

            lin_tm(w_in, OFF_SV, 256, xT_lhs, evac_v, nsub=ntok // 128)

        load_x_tile(xhalo, 128)
        transpose_to_xT(ns=1)
        P.op("dve", lambda h: h.memset(v_carry[:, :, 64:65], 1.0), writes=[B_vc])
        swa_kv_proj(128, lambda oc: kT_carry[:, oc, :], [B_kc], lambda sub: (v_carry[:, :, 0:64], [B_vc]))

        sgA, bsgA = RX.view(0, [16, 512], BF16)
        sgB, bsgB = RX.view(16384, [16, 512], BF16)
        mv, b_mv = smallv("mv", 16, 2)
        lrs, b_lrs = smallv("lrs", 18)
        nmr, b_nmr = smallv("nmr", 19)

        mv4, b_mv4 = smallv("mv4", 48, 8)
        lrs4, b_lrs4 = smallv("lrs4", 56, 4)
        nmr4, b_nmr4 = smallv("nmr4", 60, 4)
        stats4 = st.enter_context(nc.sbuf_tensor("stats4", [128, 4, 4, 6], F32))
        B_stats4 = [Buf(f"stats4_{i}") for i in range(4)]

        def ln_stats(sub, q):
            xb = xs_bufs[sub * 8:(sub + 1) * 8]
            P.op("dve", lambda h: h.bn_stats(out=stats4[:, sub, q, :], in_=xs_ap[:, sub, q * 512:(q + 1) * 512]),
                 reads=xb[q * 2:q * 2 + 2], writes=[B_stats4[sub]])

        def layer_norm_gen(g_ap, b_ap, defer_affine=False, stats_done=False):
            xbs = [xs_bufs[sub * 8:(sub + 1) * 8] for sub in range(4)]
            for sub in range(4):
                if not stats_done:
                    for q in range(4):
                        ln_stats(sub, q)
                P.op("dve", lambda h, sub=sub: h.bn_aggr(out=mv4[:, sub * 2:sub * 2 + 2],
                                                          in_=stats4[:, sub, :, :].rearrange("p a b -> p (a b)")),
                     reads=[B_stats4[sub]], writes=[b_mv4])
                if sub % 2 == 1:
                    yield
            act(lrs4, mv4[:, 1::2], AF.Ln, [b_mv4], [b_lrs4], bias=LN_EPS)
            act(lrs4, lrs4, AF.Exp, [b_lrs4], [b_lrs4], scale=-0.5)
            stt(nmr4, mv4[:, 0::2], -1.0, lrs4, ALU.mult, ALU.mult, [b_mv4, b_lrs4], [b_nmr4])
            for sub in range(4):
                if sub % 2 == 0:
                    act(xs_ap[:, sub, :], xs_ap[:, sub, :], AF.Identity, xbs[sub] + [b_lrs4, b_nmr4], xbs[sub],
                        scale=lrs4[:, sub:sub + 1], bias=nmr4[:, sub:sub + 1])
                else:
                    ts(xs_ap[:, sub, :], xs_ap[:, sub, :], lrs4[:, sub:sub + 1], nmr4[:, sub:sub + 1], ALU.mult, ALU.add,
                       xbs[sub] + [b_lrs4, b_nmr4], xbs[sub])
            yield
            if not defer_affine:
                for sub in range(4):
                    xb = xs_bufs[sub * 8:(sub + 1) * 8]
                    tt(xs_ap[:, sub, :], xs_ap[:, sub, :], g_ap, ALU.mult, xb + [B_gb], xb)
                    tt(xs_ap[:, sub, :], xs_ap[:, sub, :], b_ap, ALU.add, xb + [B_gb], xb)
                    yield

        def layer_norm_all(g_ap, b_ap, defer_affine=False, stats_done=False):
            for _ in layer_norm_gen(g_ap, b_ap, defer_affine, stats_done):
                pass

        def ln_affine(g_ap, b_ap):
            for sub in range(4):
                xb = xs_bufs[sub * 8:(sub + 1) * 8]
                tt(xs_ap[:, sub, :], xs_ap[:, sub, :], g_ap, ALU.mult, xb + [B_gb], xb)
                tt(xs_ap[:, sub, :], xs_ap[:, sub, :], b_ap, ALU.add, xb + [B_gb], xb)

        def load_gb(g, b):
            P.dma("sp", [lambda h: h.dma_start(out=gb[:, 0, :], in_=g.partition_broadcast(128)),
                         lambda h: h.dma_start(out=gb[:, 1, :], in_=b.partition_broadcast(128))],
                  B_gbload, writes=[B_gb])

        xbf_m = gb[:, :, :].rearrange("p a b -> p (a b)").bitcast(BF16).rearrange("p (s d) -> p s d", s=4)
        B_xinl = Buf("xinload")

        def prefetch_x(t):
            src = x[t * T:(t + 1) * T, :]
            P.dma("pool", [lambda h, s_=s_: h.dma_start(out=xbf_m[:, s_, :], in_=src[s_ * 128:(s_ + 1) * 128, :]) for s_ in range(4)],
                  B_xinl, writes=[B_gb])
            banks = [(psT[:, :], B_psT), (psA_bf[0], B_psA[0]), (psA_bf[1], B_psA[1]), (psA_bf[2], B_psA[2])]
            for cp2 in range(8):
                bk, bbk = banks[cp2 % 4]
                for cc in range(2):
                    c = cp2 * 2 + cc
                    for s_ in range(4):
                        tr(bk[:, cc * 512 + s_ * 128: cc * 512 + (s_ + 1) * 128], xbf_m[:, s_, c * 128:(c + 1) * 128], ident_b[:],
                           reads=[B_gb, B_ident_b], writes=[bbk], sig=(cc == 1 and s_ == 3))
                cp_rr(xT[:, cp2 * 2:cp2 * 2 + 2, :], bk.rearrange("p (a b) -> p a b", a=2), [bbk],
                      [B_xT[cp2 * 2], B_xT[cp2 * 2 + 1]])

        prefetch_x(0)
        gla_pre()
        pending_ln2 = None

        def ln2_store_gen(t):
            gen = layer_norm_gen(gb[:, 0, :], gb[:, 1, :], defer_affine=True, stats_done=True)
            for _ in gen:
                yield
            for sub in range(4):
                xb = xs_bufs[sub * 8:(sub + 1) * 8]
                tt(xs_ap[:, sub, :], xs_ap[:, sub, :], gb[:, 0, :], ALU.mult, xb + [B_gb], xb)
                tt(xs_ap[:, sub, :], xs_ap[:, sub, :], gb[:, 1, :], ALU.add, xb + [B_gb], xb)
                P.dma("sp", [lambda h, sub=sub: h.dma_start(out=out[t * T + sub * 128:t * T + (sub + 1) * 128, :], in_=xs_ap[:, sub, :])],
                      B_xstore4[sub], reads=xb)
                yield

        for t in range(NT_MAIN):
            xt_src = x[t * T:(t + 1) * T, :]
            st2_3 = gla_tile(pending_ln2)
            pending_ln2 = None

            def swa_kvproj_gen():
                P.op("dve", lambda h: h.memset(v_aug[:, :, :, 64:65], 1.0), writes=bvaug)

                def evac_k(oc, ps, bps):
                    cp_rr(kT2[:, oc, 128:640], ps[:, :], [bps], bkT2)

                yield from lin_fm_gen(w_in, k_pieces, 4, xT_rhs, evac_k)

                def evac_v(sub, ps, bps):
                    cp_rr(v_aug[:, 1 + sub, :, 0:64], ps[:, 0:256].rearrange("p (a b) -> p a b", a=4), [bps], bvaug)

                yield from lin_tm_gen(w_in, OFF_SV, 256, xT_lhs, evac_v)

            def swa_qproj(kv):
                mskv, bmskv = msk2[kv % 2]
                P.dma("sp", [lambda h: h.dma_start(out=mskv, in_=swa_mask[:, kv, :])], B_mskl[kv % 2], writes=bmskv)
                qd, bqd = qT_s2[kv % 2]
                for kb in range(2):
                    sl, bf = wget(w_in, kb * 8, [(OFF_SQ + kv * 512, 512)])
                    for oc in range(4):
                        for k in range(8):
                            kk = kb * 8 + k
                            mm(psM[oc][:, :], sl[:, k, oc * 128:(oc + 1) * 128], xT[:, kk, :], kk == 0, kk == 15,
                               reads=bf[k:k + 1] + [B_xT[kk]], writes=[B_psM[oc]], sig=(kk == 15) or (k == 7 and oc == 3))
                        if kb == 1:
                            cp_rr(qd[:, oc, :], psM[oc][:, :], [B_psM[oc]], bqd[oc:oc + 1])
                        yield

            def swa_attn(kv, t=t):
                qd, bqd = qT_s2[kv % 2]
                mskv, bmskv = msk2[kv % 2]
                items = [(n, hs) for n in range(4) for hs in range(2)]

                def ph_a(i):
                    n, hs = items[i]
                    pT, bpT = pT2[i % 2]
                    qs = slice(n * 128, (n + 1) * 128)
                    for par in range(2):
                        rows = slice(par * 64, (par + 1) * 64)
                        for b_ in range(2):
                            for cc in range(2):
                                c = 2 * hs + cc
                                if n == 0 and b_ == 0:
                                    l_ap, l_b = kT_carry[rows, kv, :], [B_kc]
                                else:
                                    kb0 = (n + b_) * 128
                                    l_ap, l_b = kT2[rows, kv, kb0:kb0 + 128], bkT2
                                col = (b_ * 2 + cc) * 128
                                mm(psA[par][:, col:col + 128], l_ap, qd[rows, c, qs], True, True,
                                   reads=l_b + bqd[c:c + 1], writes=[B_psA[par]], sig=(b_ == 1 and cc == 1))
                    for par in range(2):
                        act(pT[:, par * 512:(par + 1) * 512], psA[par][:, :], AF.Exp, [B_psA[par]], bpT[par:par + 1], scale=0.125)
                    tt(pT, pT, mskv[:, hs * 1024:(hs + 1) * 1024], ALU.mult, bpT + bmskv[hs * 2:hs * 2 + 2], bpT)
                    if t == 0 and n == 0:
                        pv_ = pT.rearrange("p (a b c) -> p a b c", a=2, b=2)[:, :, 0, :]
                        ts(pv_, pv_, hval[:, 0:1], None, ALU.mult, None, bpT + [B_hval], bpT)

                def ph_b(i):
                    n, hs = items[i]
                    pT, bpT = pT2[i % 2]
                    ob, bob = obt2[i % 2]
                    if n == 0:
                        vp, vpb = v_carry[:, kv, :], [B_vc]
                    else:
                        vp, vpb = v_aug[:, n, kv, :], bvaug
                    for lg in range(4):
                        cc, par = lg // 2, lg % 2
                        c0 = par * 512 + cc * 128
                        mm(psA[2][:, lg * 65:(lg + 1) * 65], pT[:, c0:c0 + 128], vp, True, False,
                           reads=bpT + vpb, writes=[B_psA[2]], sig=False)
                        mm(psA[2][:, lg * 65:(lg + 1) * 65], pT[:, c0 + 256:c0 + 384], v_aug[:, n + 1, kv, :], False, True,
                           reads=bpT + bvaug, writes=[B_psA[2]], sig=(lg == 3))
                    pv = psA[2][:, 0:260].rearrange("p (a b) -> p a b", a=4)
                    g0 = kv * 8 + hs * 4
                    tt(rden[:, 0:4], pv[:, :, 64:65].rearrange("p a b -> p (a b)"), sinkexp[:, g0:g0 + 4], ALU.add,
                       [B_psA[2], B_sink], [b_rden])
                    P.op("dve", lambda h: h.reciprocal(out=rden[:, 0:4], in_=rden[:, 0:4]), reads=[b_rden], writes=[b_rden])
                    tt(ob.rearrange("p (a b) -> p a b", a=4), pv[:, :, 0:64],
                       rden[:, 0:4].unsqueeze(2).to_broadcast([128, 4, 64]), ALU.mult, [B_psA[2], b_rden], bob)

                def ph_c(i):
                    n, hs = items[i]
                    ob, bob = obt2[i % 2]
                    qs = slice(n * 128, (n + 1) * 128)
                    for j in range(2):
                        tr(psT[:, j * 128:(j + 1) * 128], ob[:, j * 128:(j + 1) * 128], ident_b[:],
                           reads=bob + [B_ident_b], writes=[B_psT], sig=(j == 1))
                    c0 = kv * 4 + hs * 2
                    cp("act", o_bT[:, c0:c0 + 2, qs], psT[:, 0:256].rearrange("p (a b) -> p a b", a=2),
                       [B_psT], bo_bT[c0:c0 + 2])

                for step in range(len(items) + 2):
                    if 0 <= step - 2 < len(items):
                        ph_c(step - 2)
                    if 0 <= step - 1 < len(items):
                        ph_b(step - 1)
                    if step < len(items):
                        ph_a(step)
                    yield

            def gate_gen(off, sg, bsg, cb):
                def evac_s(oc, ps, bps):
                    c = cb * 4 + oc
                    act(sg[:, c, :], ps[:, :], AF.Sigmoid, [bps], bsg[c:c + 1])

                yield from lin_fm_gen(w_in, [(off + cb * 512, 512)], 4, xT_rhs, evac_s)

            done = set()
            heavy = [("kvproj", swa_kvproj_gen(), []), ("q0", swa_qproj(0), []), ("q1", swa_qproj(1), []),
                     ("gA0", gate_gen(OFF_GA, sgA, bsgA, 0), ["st2_3"]), ("gA1", gate_gen(OFF_GA, sgA, bsgA, 1), ["st2_3"]),
                     ("q2", swa_qproj(2), ["attn0"]),
                     ("gA2", gate_gen(OFF_GA, sgA, bsgA, 2), ["st2_3"]), ("gA3", gate_gen(OFF_GA, sgA, bsgA, 3), ["st2_3"]),
                     ("q3", swa_qproj(3), ["attn1"])]
            heavy += [("gB%d" % cb, gate_gen(OFF_GB, sgB, bsgB, cb), ["st2_3"]) for cb in range(4)]
            light = [("st2_3", st2_3, []), ("attn0", swa_attn(0), ["kvproj", "q0"]), ("attn1", swa_attn(1), ["q1"]),
                     ("attn2", swa_attn(2), ["q2"]), ("attn3", swa_attn(3), ["q3"])]

            def step(q):
                if not q:
                    return False
                name, gen, req = q[0]
                if any(r not in done for r in req):
                    return False
                try:
                    next(gen)
                except StopIteration:
                    done.add(name)
                    q.pop(0)
                return True

            while heavy or light:
                a_ = step(heavy)
                b_ = step(light)
                assert a_ or b_, ("scheduling deadlock", heavy[:1], light[:1])
            cp("act", kT_carry[:, :, :], kT2[:, :, 512:640], bkT2, [B_kc])
            cp("dve", v_carry[:, :, :], v_aug[:, 4, :, :], bvaug, [B_vc])

            for cb in range(4):
                def evac_a(oc, ps, bps, cb=cb):
                    c = cb * 4 + oc
                    tt(t1[:, oc, :], ps[:, :], sgA[:, c, :], ALU.mult, [bps] + bsgA[c:c + 1], bt1[oc * 2:oc * 2 + 2])

                lin_fm(w_bg, [(cb * 512, 512)], 4,
                       lambda k: (o_aT[:, k, :], bo_aT[k:k + 1]), evac_a)

                def evac_b(oc, ps, bps, cb=cb):
                    c = cb * 4 + oc
                    tt(t2, ps[:, :], sgB[:, c, :], ALU.mult, [bps] + bsgB[c:c + 1], bt2)
                    tt(mgT[:, c, :], t2, t1[:, oc, :], ALU.add, bt2 + bt1[oc * 2:oc * 2 + 2], bmgT[c:c + 1])

                lin_fm(w_bs, [(cb * 512, 512)], 4,
                       lambda k: (o_bT[:, k, :], bo_bT[k:k + 1]), evac_b)

            load_x_tile(xt_src)
            load_gb(ln1_g, ln1_b)
            for cb in range(4):
                def evac_o(sub, ps, bps, cb=cb):
                    xb = xs_bufs[sub * 8 + cb * 2: sub * 8 + cb * 2 + 2]
                    stt(xs_ap[:, sub, cb * 512:(cb + 1) * 512], xs_ap[:, sub, cb * 512:(cb + 1) * 512], ALPHA, ps[:, :],
                        ALU.mult, ALU.add, [bps] + xb, xb)
                    ln_stats(sub, cb)

                lin_tm(w_out, cb * 512, 512,
                       lambda k, sub: (mgT[:, k, sub * 128:(sub + 1) * 128], bmgT[k:k + 1]), evac_o)
            layer_norm_all(gb[:, 0, :], gb[:, 1, :], defer_affine=True, stats_done=True)
            transpose_to_xT(affine=True)

            for cb in range(16):
                def evac_h(oc, ps, bps, cb=cb):
                    m = cb * 4 + oc
                    j = m % 2
                    act(rtmp[:, j, :], ps[:, :], AF.Relu, [bps], [B_rtmp[j]])
                    tt(hT[:, m, :], rtmp[:, j, :], rtmp[:, j, :], ALU.mult, [B_rtmp[j]], bhT[m:m + 1])

                lin_fm(w_up, [(cb * 512, 512)], 4, xT_rhs, evac_h)
                if cb == 2:
                    ln_affine(gb[:, 0, :], gb[:, 1, :])
            for cb in range(4):
                def evac_d(sub, ps, bps, cb=cb):
                    xb = xs_bufs[sub * 8 + cb * 2: sub * 8 + cb * 2 + 2]
                    stt(xs_ap[:, sub, cb * 512:(cb + 1) * 512], xs_ap[:, sub, cb * 512:(cb + 1) * 512], ALPHA, ps[:, :],
                        ALU.mult, ALU.add, [bps] + xb, xb)
                    ln_stats(sub, cb)

                lin_tm(w_dn, cb * 512, 512,
                       lambda k, sub: (hT[:, k, sub * 128:(sub + 1) * 128], bhT[k:k + 1]), evac_d, nkb=8)
                if cb == 0:
                    if t + 1 < NT_MAIN:
                        prefetch_x(t + 1)
                    load_gb(ln2_g, ln2_b)
            if t + 1 < NT_MAIN:
                gla_pre()
            pending_ln2 = ln2_store_gen(t)
            if t + 1 == NT_MAIN:
                exhaust(pending_ln2)

        P.final_wait("sp", xs_bufs)
        with nc.Block() as block:
            P.emit(block)
    return nc


def make_swa_mask():
    s = np.arange(128, dtype=np.float64)[:, None]
    q = np.arange(128, dtype=np.float64)[None, :]
    m = np.zeros((128, 4, 2, 2, 2, 2, 128), dtype=np.float64)
    for kv in range(4):
        for g in range(8):
            hh = kv * 8 + g
            slope = 2.0 ** (-8.0 * (hh + 1) / 32)
            c, par = g // 2, g % 2
            hs, cc = c // 2, c % 2
            m[:, kv, hs, par, 0, cc, :] = np.where(s > q, np.exp(-slope * (q - s + 128)), 0.0)
            m[:, kv, hs, par, 1, cc, :] = np.where(s <= q, np.exp(-slope * (q - s)), 0.0)
    return m.reshape(128, 4, 2048).astype(ml_dtypes.bfloat16)


_NC_CACHE = {}


def kernel(x, w_in, w_alpha_up, b_alpha, gla_norm_w, attn_sinks, w_branch_gla, w_branch_swa, w_out,
           ln1_g, ln1_b, w_ff_up, w_ff_down, ln2_g, ln2_b):
    x = np.ascontiguousarray(np.asarray(x, dtype=np.float32))
    B, S, _ = x.shape
    NCORE = 8
    per = NCORE // B
    SEG = S // per
    NT_MAIN = SEG // T
    NT_PRE = (per - 1) * NT_MAIN
    key = (NT_MAIN, NT_PRE)
    if key not in _NC_CACHE:
        _NC_CACHE[key] = build(NT_MAIN, NT_PRE)
    nc = _NC_CACHE[key]
    mask = make_swa_mask()
    shared = {
        "w_in": np.ascontiguousarray(np.asarray(w_in, np.float32)),
        "w_alpha_up": np.ascontiguousarray(np.asarray(w_alpha_up, np.float32)),
        "b_alpha": np.asarray(b_alpha, np.float32), "gla_norm_w": np.asarray(gla_norm_w, np.float32),
        "attn_sinks": np.asarray(attn_sinks, np.float32),
        "w_branch_gla": np.ascontiguousarray(np.asarray(w_branch_gla, np.float32)),
        "w_branch_swa": np.ascontiguousarray(np.asarray(w_branch_swa, np.float32)),
        "w_out": np.ascontiguousarray(np.asarray(w_out, np.float32)),
        "ln1_g": np.asarray(ln1_g, np.float32), "ln1_b": np.asarray(ln1_b, np.float32),
        "w_ff_up": np.ascontiguousarray(np.asarray(w_ff_up, np.float32)),
        "w_ff_down": np.ascontiguousarray(np.asarray(w_ff_down, np.float32)),
        "ln2_g": np.asarray(ln2_g, np.float32), "ln2_b": np.asarray(ln2_b, np.float32),
        "swa_mask": mask,
    }
    in_maps = []
    for c in range(NCORE):
        b, j = c // per, c % per
        m = dict(shared)
        m["x"] = x[b, j * SEG:(j + 1) * SEG]
        xp = np.zeros((max(NT_PRE, 1) * T, D), np.float32)
        if j > 0 and NT_PRE > 0:
            xp[(per - 1 - j) * SEG:] = x[b, 0:j * SEG]
        m["xprev"] = xp
        xh = np.zeros((128, D), np.float32)
        if j > 0:
            xh[:] = x[b, j * SEG - 128:j * SEG]
        m["xhalo"] = xh
        m["hvalid"] = np.full((128, 1), 1.0 if j > 0 else 0.0, np.float32)
        in_maps.append(m)
    res = run_bass_kernel_spmd(nc, in_maps, core_ids=list(range(NCORE)))
    outp = np.empty((B, S, D), np.float32)
    for c in range(NCORE):
        b, j = c // per, c % per
        outp[b, j * SEG:(j + 1) * SEG] = res.results[c]["out"]
    return outp
```
